# Optimizing a Trainium2 kernel written in Bass

```python
import jax, jax.numpy as jnp
from jax import lax
import numpy as np

D_MODEL = 1024
BATCH = 2
SEQ = 16384
DEPTH = 4

HEAD_DIM = 64
A_Q_HEADS = 4
A_KV_HEADS = 2
A_WINDOW = 128
B_HEADS = 6
B_BRANCHES = ((128, 1), (512, 4), (2048, 16))
C_WIDTH = 384
C_BLOCKS = 6
C_CONV = 4
C_EXP = 8.0
D_FF = 2816
BLOCK = 128
ROPE_THETA = 10000.0
EPS = 1e-6
SCALE = HEAD_DIM ** -0.5

A_WIDTH = A_Q_HEADS * HEAD_DIM
A_KV_WIDTH = A_KV_HEADS * HEAD_DIM
B_WIDTH = B_HEADS * HEAD_DIM
MIX_WIDTH = A_WIDTH + B_WIDTH + C_WIDTH
IN_SPLIT_SIZES = (A_WIDTH, A_KV_WIDTH, A_KV_WIDTH, B_WIDTH, B_WIDTH, B_WIDTH, C_WIDTH, C_WIDTH)
IN_COLS = A_WIDTH + 2 * A_KV_WIDTH + 3 * B_WIDTH + 2 * C_WIDTH

kernel_name = "hymba_style_swa_dilated_rglru_macaron"


def rms_norm(x, g):
    xf = x.astype(jnp.float32)
    y = xf * lax.rsqrt(jnp.mean(xf * xf, axis=-1, keepdims=True) + EPS)
    return (y * g.astype(jnp.float32)).astype(x.dtype)


def swiglu(x, w_gate, w_up, w_down):
    return (jax.nn.silu(x @ w_gate) * (x @ w_up)) @ w_down


def rope_tables(positions):
    inv = 1.0 / (ROPE_THETA ** (jnp.arange(0, HEAD_DIM, 2, dtype=jnp.float32) / HEAD_DIM))
    ang = positions.astype(jnp.float32)[..., None] * inv
    return jnp.cos(ang), jnp.sin(ang)


def apply_rope(x, cos, sin):
    x1, x2 = jnp.split(x.astype(jnp.float32), 2, axis=-1)
    c = cos[:, :, None, :]
    s = sin[:, :, None, :]
    return jnp.concatenate([x1 * c - x2 * s, x2 * c + x1 * s], axis=-1).astype(x.dtype)


def banded_attention(q, k, v, max_dist):
    n, g, L, hd = q.shape
    n_prev = -(-max_dist // BLOCK)
    nb = -(-L // BLOCK)
    Lp = nb * BLOCK
    qb = jnp.pad(q, ((0, 0), (0, 0), (0, Lp - L), (0, 0))).reshape(n, g, nb, BLOCK, hd)
    pad = ((0, 0), (n_prev * BLOCK, Lp - L), (0, 0))
    kp = jnp.pad(k, pad)
    vp = jnp.pad(v, pad)
    kb = jnp.concatenate([kp[:, j * BLOCK:j * BLOCK + Lp].reshape(n, nb, BLOCK, hd) for j in range(n_prev + 1)], axis=2)
    vb = jnp.concatenate([vp[:, j * BLOCK:j * BLOCK + Lp].reshape(n, nb, BLOCK, hd) for j in range(n_prev + 1)], axis=2)
    q_pos = jnp.arange(Lp).reshape(nb, BLOCK, 1)
    k_pos = (jnp.arange(nb)[:, None] * BLOCK + jnp.arange((n_prev + 1) * BLOCK)[None, :] - n_prev * BLOCK)[:, None, :]
    dist = q_pos - k_pos
    mask = (dist >= 0) & (dist <= max_dist) & (k_pos >= 0)
    s = jnp.einsum('ngbqd,nbkd->ngbqk', qb, kb).astype(jnp.float32) * SCALE
    s = jnp.where(mask, s, -jnp.inf)
    m = jnp.max(s, axis=-1, keepdims=True)
    p = jnp.exp(s - m)
    l = jnp.sum(p, axis=-1, keepdims=True)
    o = jnp.einsum('ngbqk,nbkd->ngbqd', p.astype(v.dtype), vb).astype(jnp.float32) / l
    lse = (m + jnp.log(l))[..., 0]
    o = o.reshape(n, g, Lp, hd)[:, :, :L].astype(v.dtype)
    lse = lse.reshape(n, g, Lp)[:, :, :L]
    return o, lse


def swa_sink_mixer(q, k, v, sinks):
    b, s = q.shape[:2]
    g = A_Q_HEADS // A_KV_HEADS
    qh = q.reshape(b, s, A_KV_HEADS, g, HEAD_DIM).transpose(0, 2, 3, 1, 4).reshape(b * A_KV_HEADS, g, s, HEAD_DIM)
    kh = k.transpose(0, 2, 1, 3).reshape(b * A_KV_HEADS, s, HEAD_DIM)
    vh = v.transpose(0, 2, 1, 3).reshape(b * A_KV_HEADS, s, HEAD_DIM)
    o, lse = banded_attention(qh, kh, vh, A_WINDOW - 1)
    sink = jnp.tile(sinks.astype(jnp.float32).reshape(A_KV_HEADS, g), (b, 1))[:, :, None]
    o = o * jax.nn.sigmoid(lse - sink)[..., None].astype(o.dtype)
    return o.reshape(b, A_KV_HEADS, g, s, HEAD_DIM).transpose(0, 3, 1, 2, 4).reshape(b, s, A_WIDTH)


def dilated_mixer(q, k, v):
    b, s, h, hd = q.shape
    outs, lses = [], []
    for window, d in B_BRANCHES:
        def gather(t):
            return t.reshape(b, s // d, d, h, hd).transpose(0, 3, 2, 1, 4).reshape(b * h * d, s // d, hd)
        o, lse = banded_attention(gather(q)[:, None], gather(k), gather(v), window // d)
        outs.append(o[:, 0].reshape(b, h, d, s // d, hd).transpose(0, 3, 2, 1, 4).reshape(b, s, h, hd))
        lses.append(lse[:, 0].reshape(b, h, d, s // d).transpose(0, 3, 2, 1).reshape(b, s, h))
    w = jax.nn.softmax(jnp.stack(lses, axis=-1), axis=-1)
    o = outs[0] * w[..., 0:1].astype(q.dtype)
    for i in range(1, len(B_BRANCHES)):
        o = o + outs[i] * w[..., i:i + 1].astype(q.dtype)
    return o.reshape(b, s, B_WIDTH)


def _lru_combine(left, right):
    a1, b1 = left
    a2, b2 = right
    return a1 * a2, a2 * b1 + b2


def rglru_mixer(xc, gate, conv_w, conv_b, w_r, b_r, w_i, b_i, lam, positions):
    b, s, c = xc.shape
    xp = jnp.pad(xc, ((0, 0), (C_CONV - 1, 0), (0, 0)))
    y = conv_b + conv_w[0] * xp[:, C_CONV - 1:C_CONV - 1 + s]
    for j in range(1, C_CONV):
        y = y + conv_w[j] * xp[:, C_CONV - 1 - j:C_CONV - 1 - j + s]
    yb = y.reshape(b, s, C_BLOCKS, c // C_BLOCKS)
    r = jax.nn.sigmoid(jnp.einsum('bshi,hij->bshj', yb, w_r) + b_r).reshape(b, s, c)
    ig = jax.nn.sigmoid(jnp.einsum('bshi,hij->bshj', yb, w_i) + b_i).reshape(b, s, c)
    log_a = -C_EXP * r.astype(jnp.float32) * jax.nn.softplus(-lam.astype(jnp.float32))
    reset = (positions == 0)[..., None]
    a = jnp.where(reset, 0.0, jnp.exp(log_a))
    mult = jnp.where(reset, 1.0, jnp.sqrt(-jnp.expm1(2.0 * log_a)))
    bx = mult * (ig * y).astype(jnp.float32)
    _, hs = lax.associative_scan(_lru_combine, (a, bx), axis=1)
    return (hs * jax.nn.gelu(gate.astype(jnp.float32))).astype(xc.dtype)


def _split_cols(proj):
    out, start = [], 0
    for size in IN_SPLIT_SIZES:
        out.append(proj[..., start:start + size])
        start += size
    return out


def setup_inputs(seed: int = 0) -> dict:
    key = jax.random.key(seed)
    ks = jax.random.split(key, 24)
    f32 = jnp.float32
    L = DEPTH
    nrm = lambda k, shape, scale: jax.random.normal(k, shape, f32) * scale
    a0 = jax.random.uniform(ks[14], (L, C_WIDTH), f32, minval=0.9, maxval=0.999)
    return {
        "x": jax.random.normal(ks[0], (BATCH, SEQ, D_MODEL), f32),
        "positions": jnp.broadcast_to(jnp.arange(SEQ, dtype=jnp.int32), (BATCH, SEQ)),
        "norm_ffn1": 1.0 + nrm(ks[1], (L, D_MODEL), 0.02),
        "ffn1_gate": nrm(ks[2], (L, D_MODEL, D_FF), D_MODEL ** -0.5),
        "ffn1_up": nrm(ks[3], (L, D_MODEL, D_FF), D_MODEL ** -0.5),
        "ffn1_down": nrm(ks[4], (L, D_FF, D_MODEL), D_FF ** -0.5),
        "norm_mix": 1.0 + nrm(ks[5], (L, D_MODEL), 0.02),
        "w_in": nrm(ks[6], (L, D_MODEL, IN_COLS), D_MODEL ** -0.5),
        "attn_sinks": nrm(ks[7], (L, A_Q_HEADS), 1.0),
        "conv_w": nrm(ks[8], (L, C_CONV, C_WIDTH), C_CONV ** -0.5),
        "conv_b": nrm(ks[9], (L, C_WIDTH), 0.01),
        "rg_w_r": nrm(ks[10], (L, C_BLOCKS, C_WIDTH // C_BLOCKS, C_WIDTH // C_BLOCKS), (C_WIDTH // C_BLOCKS) ** -0.5),
        "rg_b_r": nrm(ks[11], (L, C_BLOCKS, C_WIDTH // C_BLOCKS), 0.01),
        "rg_w_i": nrm(ks[12], (L, C_BLOCKS, C_WIDTH // C_BLOCKS, C_WIDTH // C_BLOCKS), (C_WIDTH // C_BLOCKS) ** -0.5),
        "rg_b_i": nrm(ks[13], (L, C_BLOCKS, C_WIDTH // C_BLOCKS), 0.01),
        "rg_lambda": jnp.log(a0) - jnp.log1p(-a0),
        "w_out": nrm(ks[15], (L, MIX_WIDTH, D_MODEL), MIX_WIDTH ** -0.5),
        "norm_ffn2": 1.0 + nrm(ks[16], (L, D_MODEL), 0.02),
        "ffn2_gate": nrm(ks[17], (L, D_MODEL, D_FF), D_MODEL ** -0.5),
        "ffn2_up": nrm(ks[18], (L, D_MODEL, D_FF), D_MODEL ** -0.5),
        "ffn2_down": nrm(ks[19], (L, D_FF, D_MODEL), D_FF ** -0.5),
        "norm_final": 1.0 + nrm(ks[20], (D_MODEL,), 0.02),
    }


def reference(x, positions, norm_ffn1, ffn1_gate, ffn1_up, ffn1_down, norm_mix, w_in, attn_sinks,
              conv_w, conv_b, rg_w_r, rg_b_r, rg_w_i, rg_b_i, rg_lambda, w_out,
              norm_ffn2, ffn2_gate, ffn2_up, ffn2_down, norm_final):
    b, s, _ = x.shape
    cos, sin = rope_tables(positions)
    for l in range(DEPTH):
        x = x + 0.5 * swiglu(rms_norm(x, norm_ffn1[l]), ffn1_gate[l], ffn1_up[l], ffn1_down[l])
        h = rms_norm(x, norm_mix[l])
        qa, ka, va, qb, kb, vb, xc, gc = _split_cols(h @ w_in[l])
        qa = apply_rope(qa.reshape(b, s, A_Q_HEADS, HEAD_DIM), cos, sin)
        ka = apply_rope(ka.reshape(b, s, A_KV_HEADS, HEAD_DIM), cos, sin)
        va = va.reshape(b, s, A_KV_HEADS, HEAD_DIM)
        out_a = swa_sink_mixer(qa, ka, va, attn_sinks[l])
        qb = apply_rope(qb.reshape(b, s, B_HEADS, HEAD_DIM), cos, sin)
        kb = apply_rope(kb.reshape(b, s, B_HEADS, HEAD_DIM), cos, sin)
        vb = vb.reshape(b, s, B_HEADS, HEAD_DIM)
        out_b = dilated_mixer(qb, kb, vb)
        out_c = rglru_mixer(xc, gc, conv_w[l], conv_b[l], rg_w_r[l], rg_b_r[l], rg_w_i[l], rg_b_i[l],
                            rg_lambda[l], positions)
        x = x + jnp.concatenate([out_a, out_b, out_c], axis=-1) @ w_out[l]
        x = x + 0.5 * swiglu(rms_norm(x, norm_ffn2[l]), ffn2_gate[l], ffn2_up[l], ffn2_down[l])
    return rms_norm(x, norm_final)
```

```python
import contextlib
import math
import numpy as np
import concourse.bass as bass
import concourse.mybir as mybir
from concourse.bass_utils import run_bass_kernel_spmd

F32 = mybir.dt.float32
BF16 = mybir.dt.bfloat16
I32 = mybir.dt.int32
AF = mybir.ActivationFunctionType
ALU = mybir.AluOpType

N_DMA_SEMS = 16
N_SWDMA_SEMS = 8
D_MODEL = 1024
D_FF = 2816
TT = 512
NSLOT = 3
SLOT_ELEMS = 4096
PREFETCH = 2
EPS = 1e-6
NSMALL = 50
O_G1, O_GM, O_G2, O_SINK, O_CW, O_CB, O_BR, O_BI, O_LAM = 0, 8, 16, 24, 26, 38, 41, 44, 47
M_GE, M_GT, M_LE, M_F16, M_O16, M_C16, M_ID, M_BGE, M_BGT, M_BLE = 0, 1, 2, 3, 4, 5, 6, 7, 8, 9
NMASK = 10
NEG = -30000.0
MAGIC = 12582912.0
TWO_PI = 2.0 * math.pi
C1 = 6.28125
C2 = TWO_PI - C1
PI_SAFE = 3.1415925


class Prog:
    COMPUTE = ("pe", "act", "dve", "pool")

    def __init__(self, nc):
        self.nc = nc
        self.ops = []
        self.eng = {"pe": nc.tensor, "act": nc.scalar, "dve": nc.vector,
                    "pool": nc.gpsimd, "sp": nc.sync}

    def op(self, eng, fn, reads=(), writes=()):
        self.ops.append(dict(eng=eng, fn=fn, reads=tuple(reads), writes=tuple(writes), dma=False))

    def dma(self, queue, fn, reads=(), writes=()):
        self.ops.append(dict(eng=queue, fn=fn, reads=tuple(reads), writes=tuple(writes), dma=True))

    def analyse(self):
        last_w = {}
        readers = {}
        ops = self.ops
        for i, o in enumerate(ops):
            deps = set()
            for r in o["reads"]:
                w = last_w.get(r)
                if w is not None:
                    deps.add(w)
            for w_ in o["writes"]:
                w = last_w.get(w_)
                if w is not None:
                    deps.add(w)
                for r in readers.get(w_, ()):
                    deps.add(r)
            deps.discard(i)
            if o["eng"] == "pe" and not o["dma"]:
                deps = {d for d in deps if not (ops[d]["eng"] == "pe" and not ops[d]["dma"])}
            o["deps"] = deps
            for r in o["reads"]:
                if r not in o["writes"]:
                    readers.setdefault(r, []).append(i)
            for w_ in o["writes"]:
                last_w[w_] = i
                readers[w_] = []
        for o in ops:
            o["signal"] = False
        for o in ops:
            for d in o["deps"]:
                ops[d]["signal"] = True
        cnt = {e: 0 for e in self.COMPUTE}
        ndma = {"dma": 0, "swdma": 0}
        dma_cnt = {"dma": [0] * N_DMA_SEMS, "swdma": [0] * N_SWDMA_SEMS}
        for o in ops:
            if o["dma"]:
                kind = "swdma" if o["eng"] == "pool" else "dma"
                s = ndma[kind] % len(dma_cnt[kind])
                ndma[kind] += 1
                dma_cnt[kind][s] += 16
                o["sem"] = (kind, s)
                o["count"] = dma_cnt[kind][s]
            elif o["signal"]:
                cnt[o["eng"]] += 1
                o["sem"] = ("eng", o["eng"])
                o["count"] = cnt[o["eng"]]
        self.final_dma = dma_cnt
        waited = {}
        for o in ops:
            e = o["eng"]
            wl = {}
            for d in o["deps"]:
                do = ops[d]
                k = do["sem"]
                wl[k] = max(wl.get(k, 0), do["count"])
            if o["dma"] and o["count"] > 16:
                k = o["sem"]
                wl[k] = max(wl.get(k, 0), o["count"] - 16)
            out = []
            for k, v in wl.items():
                if waited.get((e, k), 0) >= v:
                    continue
                waited[(e, k)] = v
                out.append((k, v))
            o["waits"] = out

    def emit(self, sems_eng, sems_dma, block):
        self.analyse()
        ops = self.ops

        def semobj(k):
            if k[0] == "dma":
                return sems_dma[k[1]]
            if k[0] == "swdma":
                return sems_dma[N_DMA_SEMS + k[1]]
            return sems_eng[k[1]]

        def run(engname):
            e = self.eng[engname]
            for o in ops:
                if o["eng"] != engname:
                    continue
                for k, v in o["waits"]:
                    e.wait_ge(semobj(k), v)
                ins = o["fn"]()
                if o["dma"]:
                    ins.then_inc(semobj(o["sem"]), 16)
                elif o["signal"]:
                    ins.then_inc(semobj(o["sem"]), 1)
            if engname == "sp":
                for kind in ("dma", "swdma"):
                    for s, v in enumerate(self.final_dma[kind]):
                        if v > 0:
                            e.wait_ge(semobj((kind, s)), v)

        @block.sync
        def _(sync):
            run("sp")

        @block.tensor
        def _(t):
            run("pe")

        @block.scalar
        def _(t):
            run("act")

        @block.vector
        def _(t):
            run("dve")

        @block.gpsimd
        def _(t):
            run("pool")


def layer_units(li):
    u = []
    for f in (0, 1):
        if f == 1:
            u.append(("wv", li))
            for q in range(6):
                u.append(("wq", li, q))
            for q in range(2):
                u.append(("wo", li, q))
        for half in range(2):
            for jl in range(11):
                u.append(("gu", li, f, half * 11 + jl))
            for mp in range(4):
                u.append(("wd", li, f, half, mp))
    return u


def build_program(S, NL, first_from_input=True, final_norm=True):
    nc = bass.Bass("TRN2", target_bir_lowering=False)
    NT = S // TT
    dt_in = lambda name, shape, dt=F32: nc.dram_tensor(name, shape, dt, kind="ExternalInput").ap()
    xT = dt_in("xT", [D_MODEL, S])
    pos = dt_in("pos", [1, S], I32)
    wgu = dt_in("wgu", [NL, 2, 22, 128, 2048])
    wd = dt_in("wd", [NL, 2, 2, 4, 128, 2816])
    wq = dt_in("wq", [NL, 6, 128, 4096])
    wv = dt_in("wv", [NL, 128, 4096])
    wo = dt_in("wo", [NL, 2, 128, 4096])
    wrg = dt_in("wrg", [NL, 128, 768])
    smalls = dt_in("smalls", [NL, 128, NSMALL])
    consts = dt_in("consts", [128, 10])
    masks = dt_in("masks", [128, NMASK * 512])
    outT = nc.dram_tensor("outT", [D_MODEL, S], F32, kind="ExternalOutput").ap()
    dt_sc = lambda name, shape, dt=BF16: nc.dram_tensor(name, shape, dt, kind="Internal").ap()
    wgu_s = dt_sc("wgu_s", [NL, 2, 22, 128, 2048])
    wd_s = dt_sc("wd_s", [NL, 2, 2, 4, 128, 2816])
    wq_s = dt_sc("wq_s", [NL, 6, 128, 4096])
    wv_s = dt_sc("wv_s", [NL, 128, 4096])
    wo_s = dt_sc("wo_s", [NL, 2, 128, 4096])
    xs = dt_sc("xs", [D_MODEL, S], F32) if NL > 1 else None

    xT_v = xT.rearrange("(c p) t -> p c t", p=128)
    outT_v = outT.rearrange("(c p) t -> p c t", p=128)
    xs_v = xs.rearrange("(c p) t -> p c t", p=128) if xs is not None else None

    P = Prog(nc)
    with contextlib.ExitStack() as es:
        def sb(name, shape, dt=BF16):
            return es.enter_context(nc.sbuf_tensor(name, shape, dt))

        x32 = sb("x32", [128, 2, 8, TT], F32)
        xn = sb("xn", [128, 8, TT])
        hbuf = sb("hbuf", [128, 11, TT])
        sg = sb("sg", [128, 2, TT])
        wslot = sb("wslot", [128, NSLOT, SLOT_ELEMS])
        Q = sb("Q", [128, 5, TT])
        Kcur = sb("Kcur", [128, 4, TT])
        Kring = sb("Kring", [128, 4, 4 * TT])
        V1 = sb("V1", [128, 5, 512])
        V4 = sb("V4", [128, 5, 4, 384])
        cosT = sb("cosT", [128, TT], F32)
        sinT = sb("sinT", [128, TT], F32)
        posI = sb("posI", [128, TT], I32)
        posF = sb("posF", [128, TT], F32)
        scr = sb("scr", [128, 10, TT], F32)
        PT = sb("PT", [128, 2, 5, TT])
        PTX = sb("PTX", [128, 2, 2, 2, TT])
        mix = sb("mix", [128, 8, TT])
        xcb = sb("xcb", [128, 3, TT + 3], F32)
        ybf = sb("ybf", [128, TT])
        gl = sb("gl", [128, 3, TT])
        hstate = sb("hstate", [128, 3], F32)
        masks_sb = sb("masks_sb", [128, NMASK, 512])
        ones_bf = sb("ones_bf", [128, 128])
        ones32 = sb("ones32", [128, 128], F32)
        smalls_sb = sb("smalls_sb", [128, NL, NSMALL], F32)
        derived = sb("derived", [128, NL, 12], F32)
        consts_sb = sb("consts_sb", [128, 10], F32)
        wrg_sb = sb("wrg_sb", [128, NL, 768])
        banks = [es.enter_context(nc.psum_tensor("bank%d" % i, [128, 512], F32)) for i in range(8)]
        sems_eng = {e: es.enter_context(nc.semaphore("s_" + e)) for e in Prog.COMPUTE}
        sems_dma = [es.enter_context(nc.semaphore("d%d" % i)) for i in range(N_DMA_SEMS + N_SWDMA_SEMS)]
        block = es.enter_context(nc.Block())

        bank_ctr = [0]

        def bank():
            i = bank_ctr[0] % 8
            bank_ctr[0] += 1
            return banks[i], ("bank", i)

        sbank_ctr = [0]

        def sbank():
            i = sbank_ctr[0] % 6
            sbank_ctr[0] += 1
            return banks[i], ("bank", i)

        def grp16(ap, a):
            return ap.rearrange("p (i a r) -> p a r i", a=4, r=4)[:, a]

        def MM(out, lhsT, rhs, start, stop, reads, writes):
            P.op("pe", lambda: nc.tensor.matmul(out, lhsT, rhs, start=start, stop=stop), reads, writes)

        def ACT(out, in_, func, reads, writes, scale=1.0, bias=None):
            if bias is None:
                P.op("act", lambda: nc.scalar.activation(out=out, in_=in_, func=func, scale=scale), reads, writes)
            else:
                P.op("act", lambda: nc.scalar.activation(out=out, in_=in_, func=func, scale=scale, bias=bias), reads, writes)

        def ENG(e):
            return {"dve": nc.vector, "pool": nc.gpsimd}[e]

        def TTo(e, out, in0, in1, op, reads, writes):
            P.op(e, lambda: ENG(e).tensor_tensor(out=out, in0=in0, in1=in1, op=op), reads, writes)

        def TS(e, out, in0, s1, s2, op0, op1, reads, writes):
            if op1 is None:
                P.op(e, lambda: ENG(e).tensor_scalar(out, in0, s1, None, op0), reads, writes)
            else:
                P.op(e, lambda: ENG(e).tensor_scalar(out, in0, s1, s2, op0, op1), reads, writes)

        def STT(e, out, in0, scalar, in1, op0, op1, reads, writes):
            P.op(e, lambda: ENG(e).scalar_tensor_tensor(out=out, in0=in0, scalar=scalar, in1=in1, op0=op0, op1=op1), reads, writes)

        def CP(e, out, in_, reads, writes):
            if e == "act":
                P.op("act", lambda: nc.scalar.copy(out, in_), reads, writes)
            else:
                P.op(e, lambda: ENG(e).tensor_copy(out, in_), reads, writes)

        def SCR(i):
            return scr[:, i, :], ("scr", i)

        def strided(ap, d, r):
            if d == 1:
                return ap
            return ap.rearrange("p (i r) -> p r i", r=d)[:, r, :]

        seq = []
        for li in range(NL):
            for n in range(NT):
                seq.extend(layer_units(li))
        wstate = dict(issued=0, used=0)

        def unit_src(u):
            k = u[0]
            if k == "gu":
                return wgu_s[u[1], u[2], u[3]], 2048
            if k == "wd":
                return wd_s[u[1], u[2], u[3], u[4]], 2816
            if k == "wq":
                return wq_s[u[1], u[2]], 4096
            if k == "wv":
                return wv_s[u[1]], 4096
            if k == "wo":
                return wo_s[u[1], u[2]], 4096
            raise KeyError(k)

        def issue_upto(idx):
            while wstate["issued"] <= idx and wstate["issued"] < len(seq):
                i = wstate["issued"]
                u = seq[i]
                src, ne = unit_src(u)
                s = i % NSLOT
                P.dma("sp", (lambda s=s, src=src, ne=ne: nc.sync.dma_start(out=wslot[:, s, 0:ne], in_=src)),
                      [("ws", u)], [("wslot", s)])
                wstate["issued"] += 1

        def Wnext(u):
            i = wstate["used"]
            assert seq[i] == u, (seq[i], u)
            issue_upto(i + PREFETCH)
            wstate["used"] += 1
            s = i % NSLOT
            return wslot[:, s, :], ("wslot", s)

        P.dma("sp", lambda: nc.sync.dma_start(out=consts_sb[:], in_=consts[:, :]), [], ["consts"])
        P.dma("sp", lambda: nc.sync.dma_start(out=smalls_sb[:], in_=smalls.rearrange("l p k -> p l k")), [], ["smalls"])
        P.dma("pool", lambda: nc.gpsimd.dma_start(out=masks_sb[:], in_=masks.rearrange("p (m k) -> p m k", m=NMASK)), [], ["masks"])
        P.dma("pool", lambda: nc.gpsimd.dma_start(out=wrg_sb[:], in_=wrg.rearrange("l p k -> p l k")), [], ["wrg"])
        P.op("pool", lambda: nc.gpsimd.memset(ones_bf[:], 1.0), [], ["ones_bf"])
        P.op("pool", lambda: nc.gpsimd.memset(ones32[:], 1.0), [], ["ones32"])
        def cast_unit(u):
            k = u[0]
            src = {"gu": lambda: wgu[u[1], u[2], u[3]], "wd": lambda: wd[u[1], u[2], u[3], u[4]],
                   "wq": lambda: wq[u[1], u[2]], "wv": lambda: wv[u[1]], "wo": lambda: wo[u[1], u[2]]}[k]()
            dst, _ = unit_src(u)
            P.dma("pool", (lambda dst=dst, src=src: nc.gpsimd.dma_start(out=dst, in_=src)), [], [("ws", u)])

        for u in layer_units(0):
            cast_unit(u)
        CAST_PER_TILE = 4
        sm = smalls_sb
        ACT(derived[:, :, 0:2], sm[:, :, O_SINK:O_SINK + 2], AF.Exp, ["smalls"], ["derived"])
        ACT(derived[:, :, 2:5], sm[:, :, O_LAM:O_LAM + 3], AF.Exp, ["smalls"], ["derived"], scale=-1.0)
        ACT(derived[:, :, 5:8], derived[:, :, 2:5], AF.Ln, ["derived"], ["derived"], bias=1.0)
        TS("dve", derived[:, :, 8:11], derived[:, :, 5:8], -16.0, None, ALU.mult, None, ["derived"], ["derived"])
        TS("dve", derived[:, :, 5:8], derived[:, :, 5:8], -8.0, None, ALU.mult, None, ["derived"], ["derived"])

        xcur = [0]

        def X(c):
            return x32[:, xcur[0], c, :], ("x", xcur[0], c)

        def XKEYS(b):
            return [("x", b, c) for c in range(8)]

        def rmsnorm(li, goff, final=False):
            sskey = None
            ssb, sskey = bank()
            for c in range(8):
                sq, sqk = PT[:, c // 5, c % 5, :], ("pt", c // 5, c % 5)
                xa, xk = X(c)
                if c % 2 == 0:
                    ACT(sq, xa, AF.Square, [xk], [sqk])
                else:
                    TTo("dve", sq, xa, xa, ALU.mult, [xk], [sqk])
                MM(ssb[:, :], ones_bf[:, :], sq, c == 0, c == 7, ["ones_bf", sqk], [sskey])
            rs, rsk = SCR(2)
            rstd, rstdk = SCR(3)
            ACT(rs, ssb[:, :], AF.Ln, [sskey], [rsk], scale=1.0 / D_MODEL, bias=EPS)
            ACT(rstd, rs, AF.Exp, [rsk], [rstdk], scale=-0.5)
            for c in range(8):
                xa, xk = X(c)
                e = "dve"
                if final:
                    STT(e, xa, xa, consts_sb[:, 2 + c:3 + c], rstd, ALU.mult, ALU.mult,
                        [xk, rstdk, "consts"], [xk])
                else:
                    STT(e, xn[:, c, :], xa, sm[:, li, goff + c:goff + c + 1], rstd, ALU.mult, ALU.mult,
                        [xk, rstdk, "smalls"], [("xn", c)])

        def ffn(li, f):
            for half in range(2):
                for jl in range(11):
                    j = half * 11 + jl
                    wsl, wk = Wnext(("gu", li, f, j))
                    wv_ = wsl[:, 0:2048].rearrange("p (g k m) -> p g k m", g=2, k=8)
                    gb, gk = bank()
                    ub, uk = bank()
                    for kc in range(8):
                        MM(gb[:, :], wv_[:, 0, kc, :], xn[:, kc, :], kc == 0, kc == 7, [wk, ("xn", kc)], [gk])
                    for kc in range(8):
                        MM(ub[:, :], wv_[:, 1, kc, :], xn[:, kc, :], kc == 0, kc == 7, [wk, ("xn", kc)], [uk])
                    ACT(sg[:, jl % 2, :], gb[:, :], AF.Silu, [gk], [("sg", jl % 2)])
                    TTo("dve", hbuf[:, jl, :], ub[:, :], sg[:, jl % 2, :], ALU.mult, [uk, ("sg", jl % 2)], [("h", jl)])
                for mp in range(4):
                    wsl, wk = Wnext(("wd", li, f, half, mp))
                    wdv = wsl[:, 0:2816].rearrange("p (a j c) -> p a j c", a=2, j=11)
                    for mm_ in range(2):
                        m = 2 * mp + mm_
                        yb, yk = bank()
                        for jl in range(11):
                            MM(yb[:, :], wdv[:, mm_, jl, :], hbuf[:, jl, :], jl == 0, jl == 10, [wk, ("h", jl)], [yk])
                        xa, xk = X(m)
                        STT("dve", xa, yb[:, :], 0.5, xa, ALU.mult, ALU.add, [yk, xk], [xk])

        def rope_tables(n):
            T0 = n * TT
            P.dma("sp", lambda: nc.sync.dma_start(out=posI[:], in_=pos[0:1, T0:T0 + TT].partition_broadcast(128)),
                  [], ["posI"])
            CP("dve", posF[:], posI[:], ["posI"], ["posF"])
            ang, angk = SCR(4)
            k1, k1k = SCR(5)
            r1, r1k = SCR(6)
            r2, r2k = SCR(7)
            TS("dve", ang, posF[:], consts_sb[:, 0:1], None, ALU.mult, None, ["posF", "consts"], [angk])
            TS("dve", k1, ang, 1.0 / TWO_PI, MAGIC, ALU.mult, ALU.add, [angk], [k1k])
            TS("dve", k1, k1, MAGIC, None, ALU.subtract, None, [k1k], [k1k])
            STT("dve", r1, k1, -C1, ang, ALU.mult, ALU.add, [k1k, angk], [r1k])
            STT("dve", r2, k1, -C2, r1, ALU.mult, ALU.add, [k1k, r1k], [r2k])
            TS("dve", r1, r2, -PI_SAFE, PI_SAFE, ALU.max, ALU.min, [r2k], [r1k])
            ACT(sinT[:], r1, AF.Sin, [r1k, "consts"], ["sinT"], scale=consts_sb[:, 1:2])
            TS("dve", k1, r2, math.pi / 2, -TWO_PI, ALU.is_gt, ALU.mult, [r2k], [k1k])
            STT("dve", ang, r2, math.pi / 2, k1, ALU.add, ALU.add, [r2k, k1k], [angk])
            TS("dve", ang, ang, -PI_SAFE, PI_SAFE, ALU.max, ALU.min, [angk], [angk])
            ACT(cosT[:], ang, AF.Sin, [angk], ["cosT"])

        evac_ctr = [0]

        def evac(out, in_, reads, writes):
            e = ("act", "dve")[evac_ctr[0] % 2]
            evac_ctr[0] += 1
            CP(e, out, in_, reads, writes)

        def v_proj(li, n):
            wsl, wk = Wnext(("wv", li))
            wvv = wsl.rearrange("p (k n) -> p k n", k=8)
            for b in range(4):
                bk, bkk = bank()
                for kc in range(8):
                    MM(bk[:, :], xn[:, kc, b * 128:(b + 1) * 128], wvv[:, kc, :], kc == 0, kc == 7, [wk, ("xn", kc)], [bkk])
                evac(V1[:, b + 1, :], bk[:, :], [bkk], [("v1", b + 1)])
            for r in range(4):
                bk, bkk = bank()
                for kc in range(8):
                    MM(bk[:, 0:384], strided(xn[:, kc, :], 4, r), wvv[:, kc, 128:512], kc == 0, kc == 7, [wk, ("xn", kc)], [bkk])
                evac(V4[:, n % 5, r, :], bk[:, 0:384], [bkk], [("v4", n % 5)])

        QORD = [("rope", "q", 0), ("rope", "q", 1), ("rope", "q", 2), ("rope", "k", 0), ("rope", "k", 1), ("rope", "k", 2),
                ("rope", "q", 3), ("rope", "q", 4), ("rope", "k", 3), ("xc", 0), ("xc", 1), ("xc", 2), ("gc", 0), ("gc", 1), ("gc", 2)]

        def qk_proj(li, n):
            work = []
            for ent in QORD:
                if ent[0] == "rope":
                    work.append((ent, 0))
                    work.append((ent, 1))
                else:
                    work.append((ent, 0))
            assert len(work) == 24
            pend = None
            for oc_g, (ent, part) in enumerate(work):
                u, oc = divmod(oc_g, 4)
                if oc == 0:
                    wsl, wk = Wnext(("wq", li, u))
                    wqv = wsl.rearrange("p (o k m) -> p o k m", o=4, k=8)
                bk, bkk = bank()
                for kc in range(8):
                    MM(bk[:, :], wqv[:, oc, kc, :], xn[:, kc, :], kc == 0, kc == 7, [wk, ("xn", kc)], [bkk])
                if ent[0] == "rope":
                    if part == 0:
                        t1, t1k = SCR(6)
                        TTo("dve", t1, bk[:, :], cosT[:], ALU.mult, [bkk, "cosT"], [t1k])
                    else:
                        t1, t1k = SCR(6)
                        t2, t2k = SCR(7)
                        TTo("dve", t2, bk[:, :], sinT[:], ALU.mult, [bkk, "sinT"], [t2k])
                        if ent[1] == "q":
                            dst, dk = Q[:, ent[2], :], ("q", ent[2])
                        else:
                            dst, dk = Kcur[:, ent[2], :], ("kcur", ent[2])
                        TTo("pool", dst, t1, t2, ALU.add, [t1k, t2k], [dk])
                elif ent[0] == "xc":
                    c = ent[1]
                    CP("act", xcb[:, c, 3:TT + 3], bk[:, :], [bkk], [("xcb", c)])
                else:
                    c = ent[1]
                    gx, gxk = SCR(0)
                    u2, u2k = SCR(1)
                    CP("act", gx, bk[:, :], [bkk], [gxk])
                    ACT(u2, bk[:, :], AF.Square, [bkk], [u2k])
                    TS("dve", u2, u2, 0.044715, 1.0, ALU.mult, ALU.add, [u2k], [u2k])
                    TTo("dve", u2, u2, gx, ALU.mult, [u2k, gxk], [u2k])
                    ACT(u2, u2, AF.Sigmoid, [u2k], [u2k], scale=2.0 * math.sqrt(2.0 / math.pi))
                    TTo("pool", gl[:, c, :], gx, u2, ALU.mult, [gxk, u2k], [("gl", c)])

        ROWS = (slice(0, 64), slice(64, 128))

        def attn_stage1(n, unit, pb):
            kind, qi, ki, d = unit["kind"], unit["qi"], unit["ki"], unit["d"]
            G = 4
            QB = TT // G
            isA = kind == "A"
            bP = [sbank(), sbank()]
            bC = [sbank(), sbank()]
            slot_prev = (n - 1) % 4
            have_prev = False
            have_prev_h = [False, False]
            addm = d == 1
            ident = masks_sb[:, M_ID, 0:128]
            for hh, rows in enumerate(ROWS):
                for g in range(G):
                    cols = slice(g * QB, (g + 1) * QB)
                    qap = strided(Q[rows, qi, :], d, g) if d > 1 else Q[rows, qi, cols]
                    kp, nk, kkeys = None, 0, []
                    if d == 1:
                        if g >= 1:
                            kp, nk, kkeys = Kcur[rows, ki, (g - 1) * 128:g * 128], 128, [("kcur", ki)]
                        elif n >= 1:
                            kp, nk, kkeys = Kring[rows, ki, slot_prev * TT + 384:slot_prev * TT + 512], 128, ["kring"]
                    else:
                        if n >= 1:
                            kp, nk, kkeys = strided(Kring[rows, ki, slot_prev * TT:(slot_prev + 1) * TT], 4, g), 128, ["kring"]
                    if nk:
                        if addm and not have_prev_h[hh]:
                            MM(bP[hh][0][:, :], ident, masks_sb[:, M_BGT if isA else M_BGE, :], True, False,
                               ["masks"], [bP[hh][1]])
                        have_prev = True
                        have_prev_h[hh] = True
                        MM(bP[hh][0][0:nk, cols], kp, qap, not addm, True, kkeys + [("q", qi)], [bP[hh][1]])
                    if addm and g == 0:
                        MM(bC[hh][0][:, :], ident, masks_sb[:, M_BLE, :], True, False, ["masks"], [bC[hh][1]])
                    kc_ = strided(Kcur[rows, ki, :], d, g) if d > 1 else Kcur[rows, ki, cols]
                    MM(bC[hh][0][0:QB, cols], kc_, qap, not addm, True, [("kcur", ki), ("q", qi)], [bC[hh][1]])
            cpar = unit["mixc"] % 2
            for hh in range(2):
                if have_prev:
                    ACT(PT[:, pb, hh, :], bP[hh][0][:, :], AF.Exp, [bP[hh][1]], [("pt", pb, hh)], scale=0.125)
                    if d == 4:
                        TTo("dve", PTX[:, cpar, hh, 1, :], PT[:, pb, hh, :], masks_sb[:, M_F16, :], ALU.mult,
                            [("pt", pb, hh), "masks"], [("ptx", cpar, hh, 1)])
                    mi = M_GT if isA else M_GE
                    if not addm:
                        TTo("pool", PT[:, pb, hh, :], PT[:, pb, hh, :], masks_sb[:, mi, :], ALU.mult,
                            [("pt", pb, hh), "masks"], [("pt", pb, hh)])
                ACT(PT[:, pb, 2 + hh, :], bC[hh][0][:, :], AF.Exp, [bC[hh][1]], [("pt", pb, 2 + hh)], scale=0.125)
                if d == 4:
                    TTo("dve", PTX[:, cpar, hh, 0, :], PT[:, pb, 2 + hh, :], masks_sb[:, M_C16, :], ALU.mult,
                        [("pt", pb, 2 + hh), "masks"], [("ptx", cpar, hh, 0)])
                if not addm:
                    TTo("pool", PT[:, pb, 2 + hh, :], PT[:, pb, 2 + hh, :], masks_sb[:, M_LE, :], ALU.mult,
                        [("pt", pb, 2 + hh), "masks"], [("pt", pb, 2 + hh)])

        UB, UK = banks[6], ("bank", 6)
        LB, LK = banks[7], ("bank", 7)

        def attn_stage2(n, unit, pb):
            d = unit["d"]
            G = 4
            QB = TT // G
            Ub, Uk, Lb, Lk = UB, UK, LB, LK
            for hh, rows in enumerate(ROWS):
                vc0 = unit["vcols"][hh] - (128 if d > 1 else 0)
                vcols = slice(vc0, vc0 + 64)
                for g in range(G):
                    cols = slice(g * QB, (g + 1) * QB)
                    vp, nk, vkeys = None, 0, []
                    if d == 1:
                        if g >= 1 or n >= 1:
                            vp, nk, vkeys = V1[:, g, vcols], 128, [("v1", g)]
                        vc, vck = V1[:, g + 1, vcols], [("v1", g + 1)]
                    else:
                        if n >= 1:
                            vp, nk, vkeys = V4[:, (n - 1) % 5, g, vcols], 128, [("v4", (n - 1) % 5)]
                        vc, vck = V4[:, n % 5, g, vcols], [("v4", n % 5)]
                    if nk:
                        MM(Ub[rows, cols], vp, PT[0:nk, pb, hh, cols], True, False, vkeys + [("pt", pb, hh)], [Uk])
                    MM(Ub[rows, cols], vc, PT[0:QB, pb, 2 + hh, cols], not nk, True, vck + [("pt", pb, 2 + hh)], [Uk])
                if d == 1 and n == 0:
                    MM(Lb[rows, 0:128], ones_bf[0:128, 0:64], PT[0:128, pb, 2 + hh, 0:128], True, True,
                       ["ones_bf", ("pt", pb, 2 + hh)], [Lk])
                    MM(Lb[rows, 128:512], ones_bf[0:128, 0:64], PT[0:128, pb, hh, 128:512], True, False,
                       ["ones_bf", ("pt", pb, hh)], [Lk])
                    MM(Lb[rows, 128:512], ones_bf[0:128, 0:64], PT[0:128, pb, 2 + hh, 128:512], False, True,
                       ["ones_bf", ("pt", pb, 2 + hh)], [Lk])
                else:
                    nkp = 0 if n == 0 else 128
                    if nkp:
                        MM(Lb[rows, :], ones_bf[0:nkp, 0:64], PT[0:nkp, pb, hh, :], True, False,
                           ["ones_bf", ("pt", pb, hh)], [Lk])
                    MM(Lb[rows, :], ones_bf[0:QB, 0:64], PT[0:QB, pb, 2 + hh, :], not nkp, True,
                       ["ones_bf", ("pt", pb, 2 + hh)], [Lk])
            return (Ub, Uk), (Lb, Lk)

        def ages16(n):
            return [k for k in (4, 3, 2, 1, 0) if n - k >= 0]

        def attn16_stage1(n, unit, pb):
            qi, ki, hh = unit["qi"], unit["ki"], unit["hh"]
            rows = ROWS[hh]
            for k in ages16(n):
                if k < 2:
                    continue
                bk, bkk = sbank()
                for g in range(4):
                    qap = strided(Q[rows, qi, :], 4, g)
                    sl = (n - k) % 4
                    kap, kkeys = strided(Kring[rows, ki, sl * TT:(sl + 1) * TT], 4, g), ["kring"]
                    MM(bk[:, g * 128:(g + 1) * 128], kap, qap, True, True, kkeys + [("q", qi)], [bkk])
                ACT(PT[:, pb, k, :], bk[:, :], AF.Exp, [bkk], [("pt", pb, k)], scale=0.125)
                mi = M_O16 if k == 4 else M_F16
                TTo("pool", PT[:, pb, k, :], PT[:, pb, k, :], masks_sb[:, mi, :], ALU.mult,
                    [("pt", pb, k), "masks"], [("pt", pb, k)])

        def attn16_stage2(n, unit, pb):
            hh = unit["hh"]
            rows = ROWS[hh]
            vc0 = unit["vcols"][hh] - 128
            vcols = slice(vc0, vc0 + 64)
            ks = ages16(n)
            cpar = unit["mixc"] % 2

            def pt_of(k):
                if k < 2:
                    return PTX[:, cpar, hh, k, :], ("ptx", cpar, hh, k)
                return PT[:, pb, k, :], ("pt", pb, k)

            for g in range(4):
                cols = slice(g * 128, (g + 1) * 128)
                for idx, k in enumerate(ks):
                    sl = (n - k) % 5
                    pa, pk = pt_of(k)
                    MM(UB[rows, cols], V4[:, sl, g, vcols], pa[:, cols], idx == 0, idx == len(ks) - 1,
                       [("v4", sl), pk], [UK])
            for idx, k in enumerate(ks):
                pa, pk = pt_of(k)
                MM(LB[rows, :], ones_bf[:, 0:64], pa, idx == 0, idx == len(ks) - 1, ["ones_bf", pk], [LK])
            return (UB, UK), (LB, LK)

        def attention(li, n):
            units = []
            for c in range(2):
                units.append(dict(kind="A", qi=3 + c, ki=3, d=1, vcols=(0, 64), mixc=c))
            for c in range(3):
                vcs = (128 + 128 * c, 128 + 128 * c + 64)
                for d in (1, 4):
                    units.append(dict(kind="B", qi=c, ki=c, d=d, vcols=vcs, mixc=2 + c))
                for hh in range(2):
                    units.append(dict(kind="B16", qi=c, ki=c, d=16, hh=hh, vcols=vcs, mixc=2 + c))
            Uacc, Uak = SCR(8)
            Lacc, Lak = SCR(9)
            rec, reck = SCR(5)

            def s1(k):
                u = units[k]
                (attn16_stage1 if u["kind"] == "B16" else attn_stage1)(n, u, k % 2)

            s1(0)
            for k, unit in enumerate(units):
                if k + 1 < len(units):
                    s1(k + 1)
                if unit["kind"] == "B16":
                    (Ub, Uk), (Lb, Lk) = attn16_stage2(n, unit, k % 2)
                else:
                    (Ub, Uk), (Lb, Lk) = attn_stage2(n, unit, k % 2)
                mc = unit["mixc"]
                if unit["kind"] == "A":
                    ACT(rec, Lb[:, :], AF.Ln, [Lk, "derived"], [reck], bias=derived[:, li, mc:mc + 1])
                    ACT(rec, rec, AF.Exp, [reck], [reck], scale=-1.0)
                    TTo("dve", mix[:, mc, :], Ub[:, :], rec, ALU.mult, [Uk, reck], [("mix", mc)])
                elif unit["kind"] == "B":
                    d = unit["d"]
                    if d == 1:
                        CP("act", Uacc, Ub[:, :], [Uk], [Uak])
                        CP("act", Lacc, Lb[:, :], [Lk], [Lak])
                    else:
                        for (acc, acck, b_, bk_) in ((Uacc, Uak, Ub, Uk), (Lacc, Lak, Lb, Lk)):
                            av = acc.rearrange("p (i r) -> p r i", r=d)
                            bv = b_[:, :].rearrange("p (r i) -> p r i", r=d)
                            TTo("dve", av, bv, av, ALU.add, [bk_, acck], [acck])
                elif unit["hh"] == 1:
                    for (acc, acck, b_, bk_) in ((Uacc, Uak, Ub, Uk), (Lacc, Lak, Lb, Lk)):
                        av = acc.rearrange("p (i r) -> p r i", r=4)
                        bv = b_[:, :].rearrange("p (r i) -> p r i", r=4)
                        TTo("dve", av, bv, av, ALU.add, [bk_, acck], [acck])
                    ACT(rec, Lacc, AF.Ln, [Lak], [reck])
                    ACT(rec, rec, AF.Exp, [reck], [reck], scale=-1.0)
                    TTo("dve", mix[:, mc, :], Uacc, rec, ALU.mult, [Uak, reck], [("mix", mc)])
                if k == 0:
                    rglru_prep()
                if k in (0, 4, 8):
                    rglru(li, n, k // 4, 1)
                if k in (2, 6, 10):
                    rglru(li, n, (k - 2) // 4, 2)

        def rglru_prep():
            m0, m0k = SCR(4)
            nm, nmk = SCR(6)
            TS("dve", m0, posF[:], 0.0, None, ALU.is_equal, None, ["posF"], [m0k])
            TS("dve", nm, m0, -1.0, 1.0, ALU.mult, ALU.add, [m0k], [nmk])

        def rglru(li, n, c, part):
            m0, m0k = SCR(4)
            nm, nmk = SCR(6)
            if True:
                yc, yck = SCR(0)
                r_, rk = SCR(1)
                ig, igk = SCR(2)
                av, avk = SCR(3)
                mu, muk = SCR(7)
                hs, hsk = SCR(1)
                cw = lambda j: sm[:, li, O_CW + j * 3 + c:O_CW + j * 3 + c + 1]
                if part == 1:
                    ACT(yc, xcb[:, c, 3:TT + 3], AF.Identity, [("xcb", c), "smalls"], [yck], scale=cw(0),
                        bias=sm[:, li, O_CB + c:O_CB + c + 1])
                    for j in range(1, 4):
                        STT("dve", yc, xcb[:, c, 3 - j:TT + 3 - j], cw(j), yc, ALU.mult, ALU.add,
                            [("xcb", c), "smalls", yck], [yck])
                    CP("pool", xcb[:, c, 0:3], xcb[:, c, TT:TT + 3], [("xcb", c)], [("xcb", c)])
                    CP("act", ybf[:], yc, [yck], ["ybf"])
                    return
                rb, rbk = sbank()
                ib, ibk = sbank()
                MM(rb[:, :], wrg_sb[:, li, c * 128:(c + 1) * 128], ybf[:], True, True, ["wrg", "ybf"], [rbk])
                MM(ib[:, :], wrg_sb[:, li, 384 + c * 128:384 + (c + 1) * 128], ybf[:], True, True, ["wrg", "ybf"], [ibk])
                ACT(r_, rb[:, :], AF.Sigmoid, [rbk, "smalls"], [rk], bias=sm[:, li, O_BR + c:O_BR + c + 1])
                ACT(ig, ib[:, :], AF.Sigmoid, [ibk, "smalls"], [igk], bias=sm[:, li, O_BI + c:O_BI + c + 1])
                ACT(av, r_, AF.Exp, [rk, "derived"], [avk], scale=derived[:, li, 5 + c:6 + c])
                ACT(mu, r_, AF.Exp, [rk, "derived"], [muk], scale=derived[:, li, 8 + c:9 + c])
                ACT(mu, mu, AF.Sqrt, [muk], [muk], scale=-1.0, bias=1.0)
                TTo("pool", av, av, nm, ALU.mult, [avk, nmk], [avk])
                TTo("pool", mu, mu, nm, ALU.mult, [muk, nmk], [muk])
                TTo("pool", mu, mu, m0, ALU.add, [muk, m0k], [muk])
                TTo("dve", ig, ig, yc, ALU.mult, [igk, yck], [igk])
                TTo("dve", ig, ig, mu, ALU.mult, [igk, muk], [igk])
                P.op("dve", (lambda hs=hs, av=av, ig=ig, c=c: nc.vector.tensor_tensor_scan(
                    out=hs, data0=av, data1=ig, initial=hstate[:, c:c + 1], op0=ALU.mult, op1=ALU.add)),
                    [avk, igk, ("hstate", c)], [hsk])
                CP("act", hstate[:, c:c + 1], hs[:, TT - 1:TT], [hsk], [("hstate", c)])
                TTo("pool", mix[:, 5 + c, :], hs, gl[:, c, :], ALU.mult, [hsk, ("gl", c)], [("mix", 5 + c)])

        def w_out(li):
            for u in range(2):
                wsl, wk = Wnext(("wo", li, u))
                wov = wsl.rearrange("p (o k m) -> p o k m", o=4, k=8)
                for mc in range(4):
                    m = 4 * u + mc
                    bk, bkk = bank()
                    for kc in range(8):
                        MM(bk[:, :], wov[:, mc, kc, :], mix[:, kc, :], kc == 0, kc == 7, [wk, ("mix", kc)], [bkk])
                    xa, xk = X(m)
                    TTo("dve", xa, bk[:, :], xa, ALU.add, [bkk, xk], [xk])

        def state_update(n):
            s = n % 4
            CP("pool", Kring[:, :, s * TT:(s + 1) * TT], Kcur[:, :, :], [("kcur", i) for i in range(4)], ["kring"])
            CP("pool", V1[:, 0, :], V1[:, 4, :], [("v1", 4)], [("v1", 0)])

        def xload(li_, n_):
            b = (li_ * NT + n_) % 2
            src = xT_v if li_ == 0 else xs_v
            T0_ = n_ * TT
            P.dma("sp", (lambda: nc.sync.dma_start(out=x32[:, b], in_=src[:, :, T0_:T0_ + TT])),
                  [("xs", n_)], XKEYS(b))

        for li in range(NL):
            P.op("pool", lambda: nc.gpsimd.memset(hstate[:], 0.0), [("hstate", c) for c in range(3)],
                 [("hstate", c) for c in range(3)])
            P.op("pool", lambda: nc.gpsimd.memset(xcb[:, :, 0:3], 0.0), [], [("xcb", c) for c in range(3)])
            last = li == NL - 1
            for n in range(NT):
                T0 = n * TT
                gidx = li * NT + n
                xcur[0] = gidx % 2
                if li + 1 < NL:
                    nxt = layer_units(li + 1)
                    if n == NT - 1:
                        todo = nxt[n * CAST_PER_TILE:]
                    else:
                        todo = nxt[n * CAST_PER_TILE:(n + 1) * CAST_PER_TILE]
                    for u in todo:
                        cast_unit(u)
                if gidx == 0:
                    xload(0, 0)
                rmsnorm(li, O_G1)
                if gidx + 1 < NL * NT:
                    xload((gidx + 1) // NT, (gidx + 1) % NT)
                ffn(li, 0)
                rmsnorm(li, O_GM)
                rope_tables(n)
                v_proj(li, n)
                qk_proj(li, n)
                attention(li, n)
                w_out(li)
                state_update(n)
                rmsnorm(li, O_G2)
                ffn(li, 1)
                if last:
                    if final_norm:
                        rmsnorm(li, 0, final=True)
                    P.dma("sp", (lambda T0=T0, b=xcur[0]: nc.sync.dma_start(out=outT_v[:, :, T0:T0 + TT], in_=x32[:, b])),
                          XKEYS(xcur[0]), [("out", n)])
                else:
                    P.dma("sp", (lambda T0=T0, b=xcur[0]: nc.sync.dma_start(out=xs_v[:, :, T0:T0 + TT], in_=x32[:, b])),
                          XKEYS(xcur[0]), [("xs", n)])
        assert wstate["used"] == len(seq)
        P.emit(sems_eng, sems_dma, block)
    return nc


def _qk_cols():
    HD = 64
    qa0, ka0, va0, qb0, kb0, vb0, xc0, gc0 = 0, 256, 384, 512, 896, 1280, 1664, 2048

    def heads(base, hs):
        return np.concatenate([np.arange(base + h * HD, base + (h + 1) * HD) for h in hs])

    def rot(cols):
        c = cols.reshape(-1, HD)
        return np.concatenate([c[:, 32:], c[:, :32]], axis=1).reshape(-1)

    chunks = []
    for c in range(3):
        p = heads(qb0, [2 * c, 2 * c + 1])
        chunks += [p, rot(p)]
    for c in range(3):
        p = heads(kb0, [2 * c, 2 * c + 1])
        chunks += [p, rot(p)]
    for c in range(2):
        p = heads(qa0, [c, c + 2])
        chunks += [p, rot(p)]
    p = heads(ka0, [0, 1])
    chunks += [p, rot(p)]
    for c in range(3):
        chunks.append(np.arange(xc0 + c * 128, xc0 + (c + 1) * 128))
    for c in range(3):
        chunks.append(np.arange(gc0 + c * 128, gc0 + (c + 1) * 128))
    vcols = np.concatenate([np.arange(va0, va0 + 128), np.arange(vb0, vb0 + 384)])
    return np.concatenate(chunks), vcols


def _mix_rows():
    HD = 64
    a = np.concatenate([np.arange(0, 64), np.arange(128, 192), np.arange(64, 128), np.arange(192, 256)])
    return np.concatenate([a, np.arange(256, 1024)])


def _consts():
    p = np.arange(128)
    inv = (1.0 / (np.float32(10000.0) ** (np.arange(0, 64, 2, dtype=np.float32) / np.float32(64)))).astype(np.float32)
    c = np.zeros((128, 10), np.float32)
    c[:, 0] = inv[p % 32]
    c[:, 1] = np.where((p % 64) < 32, -1.0, 1.0)
    j = np.arange(128)[:, None]
    i = np.arange(128)[None, :]
    m = np.zeros((128, NMASK, 512), np.float32)
    m[:, M_GE] = np.tile((j >= i).astype(np.float32), (1, 4))
    m[:, M_GT] = np.tile((j > i).astype(np.float32), (1, 4))
    m[:, M_LE] = np.tile((j <= i).astype(np.float32), (1, 4))
    same = ((j - i) % 4) == 0
    m[:, M_F16] = np.tile(same.astype(np.float32), (1, 4))
    m[:, M_O16] = np.tile((same & (j >= i)).astype(np.float32), (1, 4))
    m[:, M_C16] = np.tile((same & (j <= i)).astype(np.float32), (1, 4))
    m[:, M_ID, 0:128] = np.eye(128, dtype=np.float32)
    for src, dst in ((M_GE, M_BGE), (M_GT, M_BGT), (M_LE, M_BLE)):
        m[:, dst] = NEG * (1.0 - m[:, src])
    return c, m.reshape(128, NMASK * 512)


def prep_weights(inp, layers):
    f32 = np.float32
    NL = len(layers)
    qcols, vcols = _qk_cols()
    mrows = _mix_rows()
    wgu = np.empty((NL, 2, 22, 128, 2, 8, 128), f32)
    wd = np.empty((NL, 2, 2, 4, 128, 2, 11, 128), f32)
    wq = np.empty((NL, 6, 128, 4, 8, 128), f32)
    wv = np.empty((NL, 128, 8, 512), f32)
    wo = np.empty((NL, 2, 128, 4, 8, 128), f32)
    wrg = np.zeros((NL, 128, 2, 3, 128), f32)
    smalls = np.zeros((NL, 128, NSMALL), f32)
    p = np.arange(128)
    for i, l in enumerate(layers):
        for f, (gn, un, dn) in enumerate((("ffn1_gate", "ffn1_up", "ffn1_down"), ("ffn2_gate", "ffn2_up", "ffn2_down"))):
            for t, nm in enumerate((gn, un)):
                w = np.asarray(inp[nm][l], f32).reshape(8, 128, 22, 128)
                wgu[i, f, :, :, t] = w.transpose(2, 1, 0, 3)
            w = np.asarray(inp[dn][l], f32).reshape(2, 11, 128, 4, 2, 128)
            wd[i, f] = w.transpose(0, 3, 2, 4, 1, 5)
        win = np.asarray(inp["w_in"][l], f32)
        w = win[:, qcols].reshape(8, 128, 6, 4, 128)
        wq[i] = w.transpose(2, 1, 3, 0, 4)
        wv[i] = win[:, vcols].reshape(8, 128, 512).transpose(1, 0, 2)
        w = np.asarray(inp["w_out"][l], f32)[mrows].reshape(8, 128, 2, 4, 128)
        wo[i] = w.transpose(2, 1, 3, 0, 4)
        for t, nm in enumerate(("rg_w_r", "rg_w_i")):
            w = np.asarray(inp[nm][l], f32)
            for c in range(3):
                wrg[i, 0:64, t, c, 0:64] = w[2 * c]
                wrg[i, 64:128, t, c, 64:128] = w[2 * c + 1]
        for off, nm in ((O_G1, "norm_ffn1"), (O_GM, "norm_mix"), (O_G2, "norm_ffn2")):
            smalls[i, :, off:off + 8] = np.asarray(inp[nm][l], f32).reshape(8, 128).T
        sk = np.asarray(inp["attn_sinks"][l], f32)
        for c in range(2):
            smalls[i, :, O_SINK + c] = sk[c + 2 * (p // 64)]
        cw = np.asarray(inp["conv_w"][l], f32)
        for j in range(4):
            smalls[i, :, O_CW + 3 * j:O_CW + 3 * j + 3] = cw[j].reshape(3, 128).T
        smalls[i, :, O_CB:O_CB + 3] = np.asarray(inp["conv_b"][l], f32).reshape(3, 128).T
        smalls[i, :, O_BR:O_BR + 3] = np.asarray(inp["rg_b_r"][l], f32).reshape(3, 128).T
        smalls[i, :, O_BI:O_BI + 3] = np.asarray(inp["rg_b_i"][l], f32).reshape(3, 128).T
        smalls[i, :, O_LAM:O_LAM + 3] = np.asarray(inp["rg_lambda"][l], f32).reshape(3, 128).T
    consts, masks = _consts()
    consts[:, 2:10] = np.asarray(inp["norm_final"], f32).reshape(8, 128).T
    return dict(wgu=wgu.reshape(NL, 2, 22, 128, 2048), wd=wd.reshape(NL, 2, 2, 4, 128, 2816),
                wq=wq.reshape(NL, 6, 128, 4096), wv=wv.reshape(NL, 128, 4096), wo=wo.reshape(NL, 2, 128, 4096),
                wrg=wrg.reshape(NL, 128, 768), smalls=smalls, consts=consts, masks=masks)


_PROG_CACHE = {}


def _get_prog(S, NL, final_norm):
    key = (S, NL, final_norm)
    if key not in _PROG_CACHE:
        _PROG_CACHE[key] = build_program(S, NL, True, final_norm)
    return _PROG_CACHE[key]


FUSED = True


def kernel(**inp):
    x = np.asarray(inp["x"], np.float32)
    B, S, D = x.shape
    positions = np.asarray(inp["positions"], np.int32)
    depth = np.asarray(inp["norm_ffn1"]).shape[0]
    xT = [np.ascontiguousarray(x[b].T) for b in range(B)]
    groups = [list(range(depth))] if FUSED else [[l] for l in range(depth)]
    for gi, layers in enumerate(groups):
        final = gi == len(groups) - 1
        nc = _get_prog(S, len(layers), final)
        w = prep_weights(inp, layers)
        in_maps = []
        for b in range(B):
            m = dict(w)
            m["xT"] = xT[b]
            m["pos"] = np.ascontiguousarray(positions[b][None, :])
            in_maps.append(m)
        res = run_bass_kernel_spmd(nc, in_maps, core_ids=list(range(B)))
        xT = [np.ascontiguousarray(res.results[b]["outT"]) for b in range(B)]
    return np.stack([xT[b].T for b in range(B)], axis=0).astype(np.float32)
```

```python
import contextlib
import math
import numpy as np
import concourse.bass as bass
import concourse.mybir as mybir
from concourse.bass_utils import run_bass_kernel_spmd

F32 = mybir.dt.float32
BF16 = mybir.dt.bfloat16
I32 = mybir.dt.int32
AF = mybir.ActivationFunctionType
ALU = mybir.AluOpType

N_DMA_SEMS = 16
N_SWDMA_SEMS = 8
D_MODEL = 1024
D_FF = 2816
TT = 512
NSLOT = 3
SLOT_ELEMS = 4096
PREFETCH = 2
EPS = 1e-6
NSMALL = 50
O_G1, O_GM, O_G2, O_SINK, O_CW, O_CB, O_BR, O_BI, O_LAM = 0, 8, 16, 24, 26, 38, 41, 44, 47
M_GE, M_GT, M_LE, M_F16, M_O16, M_C16, M_ID, M_BGE, M_BGT, M_BLE, M_BF16, M_BO16 = 0, 1, 2, 3, 4, 5, 6, 7, 8, 9, 10, 11
NMASK = 12
NEG = -30000.0
MAGIC = 12582912.0
TWO_PI = 2.0 * math.pi
C1 = 6.28125
C2 = TWO_PI - C1
PI_SAFE = 3.1415925


class Prog:
    COMPUTE = ("pe", "act", "dve", "pool")

    def __init__(self, nc):
        self.nc = nc
        self.ops = []
        self.eng = {"pe": nc.tensor, "act": nc.scalar, "dve": nc.vector,
                    "pool": nc.gpsimd, "sp": nc.sync}

    def op(self, eng, fn, reads=(), writes=()):
        self.ops.append(dict(eng=eng, fn=fn, reads=tuple(reads), writes=tuple(writes), dma=False))

    def dma(self, queue, fn, reads=(), writes=()):
        self.ops.append(dict(eng=queue, fn=fn, reads=tuple(reads), writes=tuple(writes), dma=True))

    def analyse(self):
        last_w = {}
        readers = {}
        ops = self.ops
        for i, o in enumerate(ops):
            deps = set()
            for r in o["reads"]:
                w = last_w.get(r)
                if w is not None:
                    deps.add(w)
            for w_ in o["writes"]:
                w = last_w.get(w_)
                if w is not None:
                    deps.add(w)
                for r in readers.get(w_, ()):
                    deps.add(r)
            deps.discard(i)
            if o["eng"] == "pe" and not o["dma"]:
                deps = {d for d in deps if not (ops[d]["eng"] == "pe" and not ops[d]["dma"])}
            o["deps"] = deps
            for r in o["reads"]:
                if r not in o["writes"]:
                    readers.setdefault(r, []).append(i)
            for w_ in o["writes"]:
                last_w[w_] = i
                readers[w_] = []
        for o in ops:
            o["signal"] = False
        for o in ops:
            for d in o["deps"]:
                ops[d]["signal"] = True
        cnt = {e: 0 for e in self.COMPUTE}
        ndma = {"dma": 0, "swdma": 0}
        dma_cnt = {"dma": [0] * N_DMA_SEMS, "swdma": [0] * N_SWDMA_SEMS}
        for o in ops:
            if o["dma"]:
                kind = "swdma" if o["eng"] == "pool" else "dma"
                s = ndma[kind] % len(dma_cnt[kind])
                ndma[kind] += 1
                dma_cnt[kind][s] += 16
                o["sem"] = (kind, s)
                o["count"] = dma_cnt[kind][s]
            elif o["signal"]:
                cnt[o["eng"]] += 1
                o["sem"] = ("eng", o["eng"])
                o["count"] = cnt[o["eng"]]
        self.final_dma = dma_cnt
        waited = {}
        for o in ops:
            e = o["eng"]
            wl = {}
            for d in o["deps"]:
                do = ops[d]
                k = do["sem"]
                wl[k] = max(wl.get(k, 0), do["count"])
            if o["dma"] and o["count"] > 16:
                k = o["sem"]
                wl[k] = max(wl.get(k, 0), o["count"] - 16)
            out = []
            for k, v in wl.items():
                if waited.get((e, k), 0) >= v:
                    continue
                waited[(e, k)] = v
                out.append((k, v))
            o["waits"] = out

    def emit(self, sems_eng, sems_dma, block):
        self.analyse()
        ops = self.ops

        def semobj(k):
            if k[0] == "dma":
                return sems_dma[k[1]]
            if k[0] == "swdma":
                return sems_dma[N_DMA_SEMS + k[1]]
            return sems_eng[k[1]]

        def run(engname):
            e = self.eng[engname]
            for o in ops:
                if o["eng"] != engname:
                    continue
                for k, v in o["waits"]:
                    e.wait_ge(semobj(k), v)
                ins = o["fn"]()
                if o["dma"]:
                    ins.then_inc(semobj(o["sem"]), 16)
                elif o["signal"]:
                    ins.then_inc(semobj(o["sem"]), 1)
            if engname == "sp":
                for kind in ("dma", "swdma"):
                    for s, v in enumerate(self.final_dma[kind]):
                        if v > 0:
                            e.wait_ge(semobj((kind, s)), v)

        @block.sync
        def _(sync):
            run("sp")

        @block.tensor
        def _(t):
            run("pe")

        @block.scalar
        def _(t):
            run("act")

        @block.vector
        def _(t):
            run("dve")

        @block.gpsimd
        def _(t):
            run("pool")


def layer_units(li):
    u = []
    for f in (0, 1):
        if f == 1:
            u.append(("wv", li))
            for q in range(6):
                u.append(("wq", li, q))
            for q in range(2):
                u.append(("wo", li, q))
        for half in range(2):
            for jl in range(11):
                u.append(("gu", li, f, half * 11 + jl))
            for mp in range(4):
                u.append(("wd", li, f, half, mp))
    return u


def build_program(S, NL, first_from_input=True, final_norm=True):
    nc = bass.Bass("TRN2", target_bir_lowering=False)
    NT = S // TT
    dt_in = lambda name, shape, dt=F32: nc.dram_tensor(name, shape, dt, kind="ExternalInput").ap()
    xT = dt_in("xT", [D_MODEL, S])
    pos = dt_in("pos", [1, S], I32)
    wgu = dt_in("wgu", [NL, 2, 22, 128, 2048])
    wd = dt_in("wd", [NL, 2, 2, 4, 128, 2816])
    wq = dt_in("wq", [NL, 6, 128, 4096])
    wv = dt_in("wv", [NL, 128, 4096])
    wo = dt_in("wo", [NL, 2, 128, 4096])
    wrg = dt_in("wrg", [NL, 128, 768])
    smalls = dt_in("smalls", [NL, 128, NSMALL])
    consts = dt_in("consts", [128, 10])
    masks = dt_in("masks", [128, NMASK * 512])
    outT = nc.dram_tensor("outT", [D_MODEL, S], F32, kind="ExternalOutput").ap()
    dt_sc = lambda name, shape, dt=BF16: nc.dram_tensor(name, shape, dt, kind="Internal").ap()
    wgu_s = dt_sc("wgu_s", [NL, 2, 22, 128, 2048])
    wd_s = dt_sc("wd_s", [NL, 2, 2, 4, 128, 2816])
    wq_s = dt_sc("wq_s", [NL, 6, 128, 4096])
    wv_s = dt_sc("wv_s", [NL, 128, 4096])
    wo_s = dt_sc("wo_s", [NL, 2, 128, 4096])
    xs = dt_sc("xs", [D_MODEL, S], F32) if NL > 1 else None

    xT_v = xT.rearrange("(c p) t -> p c t", p=128)
    outT_v = outT.rearrange("(c p) t -> p c t", p=128)
    xs_v = xs.rearrange("(c p) t -> p c t", p=128) if xs is not None else None

    P = Prog(nc)
    with contextlib.ExitStack() as es:
        def sb(name, shape, dt=BF16):
            return es.enter_context(nc.sbuf_tensor(name, shape, dt))

        x32 = sb("x32", [128, 2, 8, TT], F32)
        xn = sb("xn", [128, 8, TT])
        hbuf = sb("hbuf", [128, 11, TT])
        sg = sb("sg", [128, 2, TT])
        wslot = sb("wslot", [128, NSLOT, SLOT_ELEMS])
        Q = sb("Q", [128, 5, TT])
        Kcur = sb("Kcur", [128, 4, TT])
        Kring = sb("Kring", [128, 4, 4 * TT])
        V1 = sb("V1", [128, 5, 512])
        V4 = sb("V4", [128, 5, 4, 384])
        cosT = sb("cosT", [128, TT], F32)
        sinT = sb("sinT", [128, TT], F32)
        posI = sb("posI", [128, TT], I32)
        posF = sb("posF", [128, TT], F32)
        scr = sb("scr", [128, 10, TT], F32)
        PT = sb("PT", [128, 2, 5, TT])
        PTX = sb("PTX", [128, 2, 2, 2, TT])
        mix = sb("mix", [128, 8, TT])
        xcb = sb("xcb", [128, 3, TT + 3], F32)
        ybf = sb("ybf", [128, TT])
        gl = sb("gl", [128, 3, TT])
        hstate = sb("hstate", [128, 3], F32)
        masks_sb = sb("masks_sb", [128, NMASK, 512])
        ones_bf = sb("ones_bf", [128, 128])
        ones32 = sb("ones32", [128, 128], F32)
        smalls_sb = sb("smalls_sb", [128, NL, NSMALL], F32)
        derived = sb("derived", [128, NL, 12], F32)
        consts_sb = sb("consts_sb", [128, 10], F32)
        wrg_sb = sb("wrg_sb", [128, NL, 768])
        banks = [es.enter_context(nc.psum_tensor("bank%d" % i, [128, 512], F32)) for i in range(8)]
        sems_eng = {e: es.enter_context(nc.semaphore("s_" + e)) for e in Prog.COMPUTE}
        sems_dma = [es.enter_context(nc.semaphore("d%d" % i)) for i in range(N_DMA_SEMS + N_SWDMA_SEMS)]
        block = es.enter_context(nc.Block())

        bank_ctr = [0]

        def bank():
            i = bank_ctr[0] % 8
            bank_ctr[0] += 1
            return banks[i], ("bank", i)

        sbank_ctr = [0]

        def sbank():
            i = sbank_ctr[0] % 6
            sbank_ctr[0] += 1
            return banks[i], ("bank", i)

        def grp16(ap, a):
            return ap.rearrange("p (i a r) -> p a r i", a=4, r=4)[:, a]

        def MM(out, lhsT, rhs, start, stop, reads, writes):
            P.op("pe", lambda: nc.tensor.matmul(out, lhsT, rhs, start=start, stop=stop), reads, writes)

        def ACT(out, in_, func, reads, writes, scale=1.0, bias=None):
            if bias is None:
                P.op("act", lambda: nc.scalar.activation(out=out, in_=in_, func=func, scale=scale), reads, writes)
            else:
                P.op("act", lambda: nc.scalar.activation(out=out, in_=in_, func=func, scale=scale, bias=bias), reads, writes)

        def ENG(e):
            return {"dve": nc.vector, "pool": nc.gpsimd}[e]

        def TTo(e, out, in0, in1, op, reads, writes):
            P.op(e, lambda: ENG(e).tensor_tensor(out=out, in0=in0, in1=in1, op=op), reads, writes)

        def TS(e, out, in0, s1, s2, op0, op1, reads, writes):
            if op1 is None:
                P.op(e, lambda: ENG(e).tensor_scalar(out, in0, s1, None, op0), reads, writes)
            else:
                P.op(e, lambda: ENG(e).tensor_scalar(out, in0, s1, s2, op0, op1), reads, writes)

        def STT(e, out, in0, scalar, in1, op0, op1, reads, writes):
            P.op(e, lambda: ENG(e).scalar_tensor_tensor(out=out, in0=in0, scalar=scalar, in1=in1, op0=op0, op1=op1), reads, writes)

        def CP(e, out, in_, reads, writes):
            if e == "act":
                P.op("act", lambda: nc.scalar.copy(out, in_), reads, writes)
            else:
                P.op(e, lambda: ENG(e).tensor_copy(out, in_), reads, writes)

        def SCR(i):
            return scr[:, i, :], ("scr", i)

        def strided(ap, d, r):
            if d == 1:
                return ap
            return ap.rearrange("p (i r) -> p r i", r=d)[:, r, :]

        seq = []
        for li in range(NL):
            for n in range(NT):
                seq.extend(layer_units(li))
        wstate = dict(issued=0, used=0)

        def unit_src(u):
            k = u[0]
            if k == "gu":
                return wgu_s[u[1], u[2], u[3]], 2048
            if k == "wd":
                return wd_s[u[1], u[2], u[3], u[4]], 2816
            if k == "wq":
                return wq_s[u[1], u[2]], 4096
            if k == "wv":
                return wv_s[u[1]], 4096
            if k == "wo":
                return wo_s[u[1], u[2]], 4096
            raise KeyError(k)

        def issue_upto(idx):
            while wstate["issued"] <= idx and wstate["issued"] < len(seq):
                i = wstate["issued"]
                u = seq[i]
                src, ne = unit_src(u)
                s = i % NSLOT
                P.dma("sp", (lambda s=s, src=src, ne=ne: nc.sync.dma_start(out=wslot[:, s, 0:ne], in_=src)),
                      [("ws", u)], [("wslot", s)])
                wstate["issued"] += 1

        def Wnext(u):
            i = wstate["used"]
            assert seq[i] == u, (seq[i], u)
            issue_upto(i + PREFETCH)
            wstate["used"] += 1
            s = i % NSLOT
            return wslot[:, s, :], ("wslot", s)

        P.dma("sp", lambda: nc.sync.dma_start(out=consts_sb[:], in_=consts[:, :]), [], ["consts"])
        P.dma("sp", lambda: nc.sync.dma_start(out=smalls_sb[:], in_=smalls.rearrange("l p k -> p l k")), [], ["smalls"])
        P.dma("pool", lambda: nc.gpsimd.dma_start(out=masks_sb[:], in_=masks.rearrange("p (m k) -> p m k", m=NMASK)), [], ["masks"])
        P.dma("pool", lambda: nc.gpsimd.dma_start(out=wrg_sb[:], in_=wrg.rearrange("l p k -> p l k")), [], ["wrg"])
        P.op("pool", lambda: nc.gpsimd.memset(ones_bf[:], 1.0), [], ["ones_bf"])
        P.op("pool", lambda: nc.gpsimd.memset(ones32[:], 1.0), [], ["ones32"])
        def cast_unit(u):
            k = u[0]
            src = {"gu": lambda: wgu[u[1], u[2], u[3]], "wd": lambda: wd[u[1], u[2], u[3], u[4]],
                   "wq": lambda: wq[u[1], u[2]], "wv": lambda: wv[u[1]], "wo": lambda: wo[u[1], u[2]]}[k]()
            dst, _ = unit_src(u)
            P.dma("pool", (lambda dst=dst, src=src: nc.gpsimd.dma_start(out=dst, in_=src)), [], [("ws", u)])

        for u in layer_units(0):
            cast_unit(u)
        CAST_PER_TILE = 4
        sm = smalls_sb
        ACT(derived[:, :, 0:2], sm[:, :, O_SINK:O_SINK + 2], AF.Exp, ["smalls"], ["derived"])
        ACT(derived[:, :, 2:5], sm[:, :, O_LAM:O_LAM + 3], AF.Exp, ["smalls"], ["derived"], scale=-1.0)
        ACT(derived[:, :, 5:8], derived[:, :, 2:5], AF.Ln, ["derived"], ["derived"], bias=1.0)
        TS("dve", derived[:, :, 8:11], derived[:, :, 5:8], -16.0, None, ALU.mult, None, ["derived"], ["derived"])
        TS("dve", derived[:, :, 5:8], derived[:, :, 5:8], -8.0, None, ALU.mult, None, ["derived"], ["derived"])

        xcur = [0]

        def X(c):
            return x32[:, xcur[0], c, :], ("x", xcur[0], c)

        def XKEYS(b):
            return [("x", b, c) for c in range(8)]

        def rmsnorm(li, goff, final=False):
            sskey = None
            ssb, sskey = bank()
            for c in range(8):
                sq, sqk = PT[:, c // 5, c % 5, :], ("pt", c // 5, c % 5)
                xa, xk = X(c)
                if c % 2 == 0:
                    ACT(sq, xa, AF.Square, [xk], [sqk])
                else:
                    TTo("dve", sq, xa, xa, ALU.mult, [xk], [sqk])
                MM(ssb[:, :], ones_bf[:, :], sq, c == 0, c == 7, ["ones_bf", sqk], [sskey])
            rs, rsk = SCR(2)
            rstd, rstdk = SCR(3)
            ACT(rs, ssb[:, :], AF.Ln, [sskey], [rsk], scale=1.0 / D_MODEL, bias=EPS)
            ACT(rstd, rs, AF.Exp, [rsk], [rstdk], scale=-0.5)
            for c in range(8):
                xa, xk = X(c)
                e = "dve"
                if final:
                    STT(e, xa, xa, consts_sb[:, 2 + c:3 + c], rstd, ALU.mult, ALU.mult,
                        [xk, rstdk, "consts"], [xk])
                else:
                    STT(e, xn[:, c, :], xa, sm[:, li, goff + c:goff + c + 1], rstd, ALU.mult, ALU.mult,
                        [xk, rstdk, "smalls"], [("xn", c)])

        def ffn(li, f):
            for half in range(2):
                for jl in range(11):
                    j = half * 11 + jl
                    wsl, wk = Wnext(("gu", li, f, j))
                    wv_ = wsl[:, 0:2048].rearrange("p (g k m) -> p g k m", g=2, k=8)
                    gb, gk = bank()
                    ub, uk = bank()
                    for kc in range(8):
                        MM(gb[:, :], wv_[:, 0, kc, :], xn[:, kc, :], kc == 0, kc == 7, [wk, ("xn", kc)], [gk])
                    for kc in range(8):
                        MM(ub[:, :], wv_[:, 1, kc, :], xn[:, kc, :], kc == 0, kc == 7, [wk, ("xn", kc)], [uk])
                    ACT(sg[:, jl % 2, :], gb[:, :], AF.Silu, [gk], [("sg", jl % 2)])
                    TTo("dve", hbuf[:, jl, :], ub[:, :], sg[:, jl % 2, :], ALU.mult, [uk, ("sg", jl % 2)], [("h", jl)])
                for mp in range(4):
                    wsl, wk = Wnext(("wd", li, f, half, mp))
                    wdv = wsl[:, 0:2816].rearrange("p (a j c) -> p a j c", a=2, j=11)
                    for mm_ in range(2):
                        m = 2 * mp + mm_
                        yb, yk = bank()
                        for jl in range(11):
                            MM(yb[:, :], wdv[:, mm_, jl, :], hbuf[:, jl, :], jl == 0, jl == 10, [wk, ("h", jl)], [yk])
                        xa, xk = X(m)
                        STT("dve", xa, yb[:, :], 0.5, xa, ALU.mult, ALU.add, [yk, xk], [xk])

        def rope_tables(n):
            T0 = n * TT
            P.dma("sp", lambda: nc.sync.dma_start(out=posI[:], in_=pos[0:1, T0:T0 + TT].partition_broadcast(128)),
                  [], ["posI"])
            CP("dve", posF[:], posI[:], ["posI"], ["posF"])
            ang, angk = SCR(4)
            k1, k1k = SCR(5)
            r1, r1k = SCR(6)
            r2, r2k = SCR(7)
            TS("dve", ang, posF[:], consts_sb[:, 0:1], None, ALU.mult, None, ["posF", "consts"], [angk])
            TS("dve", k1, ang, 1.0 / TWO_PI, MAGIC, ALU.mult, ALU.add, [angk], [k1k])
            TS("dve", k1, k1, MAGIC, None, ALU.subtract, None, [k1k], [k1k])
            STT("dve", r1, k1, -C1, ang, ALU.mult, ALU.add, [k1k, angk], [r1k])
            STT("dve", r2, k1, -C2, r1, ALU.mult, ALU.add, [k1k, r1k], [r2k])
            TS("dve", r1, r2, -PI_SAFE, PI_SAFE, ALU.max, ALU.min, [r2k], [r1k])
            ACT(sinT[:], r1, AF.Sin, [r1k, "consts"], ["sinT"], scale=consts_sb[:, 1:2])
            TS("dve", k1, r2, math.pi / 2, -TWO_PI, ALU.is_gt, ALU.mult, [r2k], [k1k])
            STT("dve", ang, r2, math.pi / 2, k1, ALU.add, ALU.add, [r2k, k1k], [angk])
            TS("dve", ang, ang, -PI_SAFE, PI_SAFE, ALU.max, ALU.min, [angk], [angk])
            ACT(cosT[:], ang, AF.Sin, [angk], ["cosT"])

        evac_ctr = [0]

        def evac(out, in_, reads, writes):
            e = ("act", "dve")[evac_ctr[0] % 2]
            evac_ctr[0] += 1
            CP(e, out, in_, reads, writes)

        def v_proj(li, n):
            wsl, wk = Wnext(("wv", li))
            wvv = wsl.rearrange("p (k n) -> p k n", k=8)
            for b in range(4):
                bk, bkk = bank()
                for kc in range(8):
                    MM(bk[:, :], xn[:, kc, b * 128:(b + 1) * 128], wvv[:, kc, :], kc == 0, kc == 7, [wk, ("xn", kc)], [bkk])
                evac(V1[:, b + 1, :], bk[:, :], [bkk], [("v1", b + 1)])
            for r in range(4):
                bk, bkk = bank()
                for kc in range(8):
                    MM(bk[:, 0:384], strided(xn[:, kc, :], 4, r), wvv[:, kc, 128:512], kc == 0, kc == 7, [wk, ("xn", kc)], [bkk])
                evac(V4[:, n % 5, r, :], bk[:, 0:384], [bkk], [("v4", n % 5)])

        QORD = [("rope", "q", 0), ("rope", "q", 1), ("rope", "q", 2), ("rope", "k", 0), ("rope", "k", 1), ("rope", "k", 2),
                ("rope", "q", 3), ("rope", "q", 4), ("rope", "k", 3), ("xc", 0), ("xc", 1), ("xc", 2), ("gc", 0), ("gc", 1), ("gc", 2)]

        def qk_proj(li, n):
            work = []
            for ent in QORD:
                if ent[0] == "rope":
                    work.append((ent, 0))
                    work.append((ent, 1))
                else:
                    work.append((ent, 0))
            assert len(work) == 24
            pend = None
            for oc_g, (ent, part) in enumerate(work):
                u, oc = divmod(oc_g, 4)
                if oc == 0:
                    wsl, wk = Wnext(("wq", li, u))
                    wqv = wsl.rearrange("p (o k m) -> p o k m", o=4, k=8)
                bk, bkk = bank()
                for kc in range(8):
                    MM(bk[:, :], wqv[:, oc, kc, :], xn[:, kc, :], kc == 0, kc == 7, [wk, ("xn", kc)], [bkk])
                if ent[0] == "rope":
                    if part == 0:
                        t1, t1k = SCR(6)
                        TTo("dve", t1, bk[:, :], cosT[:], ALU.mult, [bkk, "cosT"], [t1k])
                    else:
                        t1, t1k = SCR(6)
                        t2, t2k = SCR(7)
                        TTo("dve", t2, bk[:, :], sinT[:], ALU.mult, [bkk, "sinT"], [t2k])
                        if ent[1] == "q":
                            dst, dk = Q[:, ent[2], :], ("q", ent[2])
                        else:
                            dst, dk = Kcur[:, ent[2], :], ("kcur", ent[2])
                        TTo("pool", dst, t1, t2, ALU.add, [t1k, t2k], [dk])
                elif ent[0] == "xc":
                    c = ent[1]
                    CP("act", xcb[:, c, 3:TT + 3], bk[:, :], [bkk], [("xcb", c)])
                else:
                    c = ent[1]
                    gx, gxk = SCR(0)
                    u2, u2k = SCR(1)
                    CP("act", gx, bk[:, :], [bkk], [gxk])
                    ACT(u2, bk[:, :], AF.Square, [bkk], [u2k])
                    TS("dve", u2, u2, 0.044715, 1.0, ALU.mult, ALU.add, [u2k], [u2k])
                    TTo("dve", u2, u2, gx, ALU.mult, [u2k, gxk], [u2k])
                    ACT(u2, u2, AF.Sigmoid, [u2k], [u2k], scale=2.0 * math.sqrt(2.0 / math.pi))
                    TTo("pool", gl[:, c, :], gx, u2, ALU.mult, [gxk, u2k], [("gl", c)])

        ROWS = (slice(0, 64), slice(64, 128))

        def attn_stage1(n, unit, pb):
            kind, qi, ki, d = unit["kind"], unit["qi"], unit["ki"], unit["d"]
            G = 4
            QB = TT // G
            isA = kind == "A"
            bP = [sbank(), sbank()]
            bC = [sbank(), sbank()]
            slot_prev = (n - 1) % 4
            have_prev = False
            have_prev_h = [False, False]
            addm = d == 1
            ident = masks_sb[:, M_ID, 0:128]
            for hh, rows in enumerate(ROWS):
                for g in range(G):
                    cols = slice(g * QB, (g + 1) * QB)
                    qap = strided(Q[rows, qi, :], d, g) if d > 1 else Q[rows, qi, cols]
                    kp, nk, kkeys = None, 0, []
                    if d == 1:
                        if g >= 1:
                            kp, nk, kkeys = Kcur[rows, ki, (g - 1) * 128:g * 128], 128, [("kcur", ki)]
                        elif n >= 1:
                            kp, nk, kkeys = Kring[rows, ki, slot_prev * TT + 384:slot_prev * TT + 512], 128, ["kring"]
                    else:
                        if n >= 1:
                            kp, nk, kkeys = strided(Kring[rows, ki, slot_prev * TT:(slot_prev + 1) * TT], 4, g), 128, ["kring"]
                    if nk:
                        if addm and not have_prev_h[hh]:
                            MM(bP[hh][0][:, :], ident, masks_sb[:, M_BGT if isA else M_BGE, :], True, False,
                               ["masks"], [bP[hh][1]])
                        have_prev = True
                        have_prev_h[hh] = True
                        MM(bP[hh][0][0:nk, cols], kp, qap, not addm, True, kkeys + [("q", qi)], [bP[hh][1]])
                    if addm and g == 0:
                        MM(bC[hh][0][:, :], ident, masks_sb[:, M_BLE, :], True, False, ["masks"], [bC[hh][1]])
                    kc_ = strided(Kcur[rows, ki, :], d, g) if d > 1 else Kcur[rows, ki, cols]
                    MM(bC[hh][0][0:QB, cols], kc_, qap, not addm, True, [("kcur", ki), ("q", qi)], [bC[hh][1]])
            cpar = unit["mixc"] % 2
            for hh in range(2):
                if have_prev:
                    ACT(PT[:, pb, hh, :], bP[hh][0][:, :], AF.Exp, [bP[hh][1]], [("pt", pb, hh)], scale=0.125)
                    if d == 4:
                        TTo("dve", PTX[:, cpar, hh, 1, :], PT[:, pb, hh, :], masks_sb[:, M_F16, :], ALU.mult,
                            [("pt", pb, hh), "masks"], [("ptx", cpar, hh, 1)])
                    mi = M_GT if isA else M_GE
                    if not addm:
                        TTo("pool", PT[:, pb, hh, :], PT[:, pb, hh, :], masks_sb[:, mi, :], ALU.mult,
                            [("pt", pb, hh), "masks"], [("pt", pb, hh)])
                ACT(PT[:, pb, 2 + hh, :], bC[hh][0][:, :], AF.Exp, [bC[hh][1]], [("pt", pb, 2 + hh)], scale=0.125)
                if d == 4:
                    TTo("dve", PTX[:, cpar, hh, 0, :], PT[:, pb, 2 + hh, :], masks_sb[:, M_C16, :], ALU.mult,
                        [("pt", pb, 2 + hh), "masks"], [("ptx", cpar, hh, 0)])
                if not addm:
                    TTo("pool", PT[:, pb, 2 + hh, :], PT[:, pb, 2 + hh, :], masks_sb[:, M_LE, :], ALU.mult,
                        [("pt", pb, 2 + hh), "masks"], [("pt", pb, 2 + hh)])

        UB, UK = banks[6], ("bank", 6)
        LB, LK = banks[7], ("bank", 7)

        def attn_stage2(n, unit, pb):
            d = unit["d"]
            G = 4
            QB = TT // G
            Ub, Uk, Lb, Lk = UB, UK, LB, LK
            for hh, rows in enumerate(ROWS):
                vc0 = unit["vcols"][hh] - (128 if d > 1 else 0)
                vcols = slice(vc0, vc0 + 64)
                for g in range(G):
                    cols = slice(g * QB, (g + 1) * QB)
                    vp, nk, vkeys = None, 0, []
                    if d == 1:
                        if g >= 1 or n >= 1:
                            vp, nk, vkeys = V1[:, g, vcols], 128, [("v1", g)]
                        vc, vck = V1[:, g + 1, vcols], [("v1", g + 1)]
                    else:
                        if n >= 1:
                            vp, nk, vkeys = V4[:, (n - 1) % 5, g, vcols], 128, [("v4", (n - 1) % 5)]
                        vc, vck = V4[:, n % 5, g, vcols], [("v4", n % 5)]
                    if nk:
                        MM(Ub[rows, cols], vp, PT[0:nk, pb, hh, cols], True, False, vkeys + [("pt", pb, hh)], [Uk])
                    MM(Ub[rows, cols], vc, PT[0:QB, pb, 2 + hh, cols], not nk, True, vck + [("pt", pb, 2 + hh)], [Uk])
                if d == 1 and n == 0:
                    MM(Lb[rows, 0:128], ones_bf[0:128, 0:64], PT[0:128, pb, 2 + hh, 0:128], True, True,
                       ["ones_bf", ("pt", pb, 2 + hh)], [Lk])
                    MM(Lb[rows, 128:512], ones_bf[0:128, 0:64], PT[0:128, pb, hh, 128:512], True, False,
                       ["ones_bf", ("pt", pb, hh)], [Lk])
                    MM(Lb[rows, 128:512], ones_bf[0:128, 0:64], PT[0:128, pb, 2 + hh, 128:512], False, True,
                       ["ones_bf", ("pt", pb, 2 + hh)], [Lk])
                else:
                    nkp = 0 if n == 0 else 128
                    if nkp:
                        MM(Lb[rows, :], ones_bf[0:nkp, 0:64], PT[0:nkp, pb, hh, :], True, False,
                           ["ones_bf", ("pt", pb, hh)], [Lk])
                    MM(Lb[rows, :], ones_bf[0:QB, 0:64], PT[0:QB, pb, 2 + hh, :], not nkp, True,
                       ["ones_bf", ("pt", pb, 2 + hh)], [Lk])
            return (Ub, Uk), (Lb, Lk)

        def ages16(n):
            return [k for k in (4, 3, 2, 1, 0) if n - k >= 0]

        def attn16_stage1(n, unit, pb):
            qi, ki, hh = unit["qi"], unit["ki"], unit["hh"]
            rows = ROWS[hh]
            for k in ages16(n):
                if k < 2:
                    continue
                bk, bkk = sbank()
                MM(bk[:, :], masks_sb[:, M_ID, 0:128], masks_sb[:, M_BO16 if k == 4 else M_BF16, :], True, False,
                   ["masks"], [bkk])
                for g in range(4):
                    qap = strided(Q[rows, qi, :], 4, g)
                    sl = (n - k) % 4
                    kap, kkeys = strided(Kring[rows, ki, sl * TT:(sl + 1) * TT], 4, g), ["kring"]
                    MM(bk[:, g * 128:(g + 1) * 128], kap, qap, False, True, kkeys + [("q", qi)], [bkk])
                ACT(PT[:, pb, k, :], bk[:, :], AF.Exp, [bkk], [("pt", pb, k)], scale=0.125)

        def attn16_stage2(n, unit, pb):
            hh = unit["hh"]
            rows = ROWS[hh]
            vc0 = unit["vcols"][hh] - 128
            vcols = slice(vc0, vc0 + 64)
            ks = ages16(n)
            cpar = unit["mixc"] % 2

            def pt_of(k):
                if k < 2:
                    return PTX[:, cpar, hh, k, :], ("ptx", cpar, hh, k)
                return PT[:, pb, k, :], ("pt", pb, k)

            for g in range(4):
                cols = slice(g * 128, (g + 1) * 128)
                for idx, k in enumerate(ks):
                    sl = (n - k) % 5
                    pa, pk = pt_of(k)
                    MM(UB[rows, cols], V4[:, sl, g, vcols], pa[:, cols], idx == 0, idx == len(ks) - 1,
                       [("v4", sl), pk], [UK])
            for idx, k in enumerate(ks):
                pa, pk = pt_of(k)
                MM(LB[rows, :], ones_bf[:, 0:64], pa, idx == 0, idx == len(ks) - 1, ["ones_bf", pk], [LK])
            return (UB, UK), (LB, LK)

        def attention(li, n):
            units = []
            for c in range(2):
                units.append(dict(kind="A", qi=3 + c, ki=3, d=1, vcols=(0, 64), mixc=c))
            for c in range(3):
                vcs = (128 + 128 * c, 128 + 128 * c + 64)
                for d in (1, 4):
                    units.append(dict(kind="B", qi=c, ki=c, d=d, vcols=vcs, mixc=2 + c))
                for hh in range(2):
                    units.append(dict(kind="B16", qi=c, ki=c, d=16, hh=hh, vcols=vcs, mixc=2 + c))
            Uacc, Uak = SCR(8)
            Lacc, Lak = SCR(9)
            rec, reck = SCR(5)

            def s1(k):
                u = units[k]
                (attn16_stage1 if u["kind"] == "B16" else attn_stage1)(n, u, k % 2)

            s1(0)
            for k, unit in enumerate(units):
                if k + 1 < len(units):
                    s1(k + 1)
                if unit["kind"] == "B16":
                    (Ub, Uk), (Lb, Lk) = attn16_stage2(n, unit, k % 2)
                else:
                    (Ub, Uk), (Lb, Lk) = attn_stage2(n, unit, k % 2)
                mc = unit["mixc"]
                if unit["kind"] == "A":
                    ACT(rec, Lb[:, :], AF.Ln, [Lk, "derived"], [reck], bias=derived[:, li, mc:mc + 1])
                    ACT(rec, rec, AF.Exp, [reck], [reck], scale=-1.0)
                    TTo("dve", mix[:, mc, :], Ub[:, :], rec, ALU.mult, [Uk, reck], [("mix", mc)])
                elif unit["kind"] == "B":
                    d = unit["d"]
                    if d == 1:
                        CP("act", Uacc, Ub[:, :], [Uk], [Uak])
                        CP("act", Lacc, Lb[:, :], [Lk], [Lak])
                    else:
                        for (acc, acck, b_, bk_) in ((Uacc, Uak, Ub, Uk), (Lacc, Lak, Lb, Lk)):
                            av = acc.rearrange("p (i r) -> p r i", r=d)
                            bv = b_[:, :].rearrange("p (r i) -> p r i", r=d)
                            TTo("dve", av, bv, av, ALU.add, [bk_, acck], [acck])
                elif unit["hh"] == 1:
                    for (acc, acck, b_, bk_) in ((Uacc, Uak, Ub, Uk), (Lacc, Lak, Lb, Lk)):
                        av = acc.rearrange("p (i r) -> p r i", r=4)
                        bv = b_[:, :].rearrange("p (r i) -> p r i", r=4)
                        TTo("dve", av, bv, av, ALU.add, [bk_, acck], [acck])
                    ACT(rec, Lacc, AF.Ln, [Lak], [reck])
                    ACT(rec, rec, AF.Exp, [reck], [reck], scale=-1.0)
                    TTo("dve", mix[:, mc, :], Uacc, rec, ALU.mult, [Uak, reck], [("mix", mc)])
                if k == 0:
                    rglru_prep()
                if k in (0, 4, 8):
                    rglru(li, n, k // 4, 1)
                if k in (2, 6, 10):
                    rglru(li, n, (k - 2) // 4, 2)

        def rglru_prep():
            m0, m0k = SCR(4)
            nm, nmk = SCR(6)
            TS("dve", m0, posF[:], 0.0, None, ALU.is_equal, None, ["posF"], [m0k])
            TS("dve", nm, m0, -1.0, 1.0, ALU.mult, ALU.add, [m0k], [nmk])

        def rglru(li, n, c, part):
            m0, m0k = SCR(4)
            nm, nmk = SCR(6)
            if True:
                yc, yck = SCR(0)
                r_, rk = SCR(1)
                ig, igk = SCR(2)
                av, avk = SCR(3)
                mu, muk = SCR(7)
                hs, hsk = SCR(1)
                cw = lambda j: sm[:, li, O_CW + j * 3 + c:O_CW + j * 3 + c + 1]
                if part == 1:
                    ACT(yc, xcb[:, c, 3:TT + 3], AF.Identity, [("xcb", c), "smalls"], [yck], scale=cw(0),
                        bias=sm[:, li, O_CB + c:O_CB + c + 1])
                    for j in range(1, 4):
                        STT("dve", yc, xcb[:, c, 3 - j:TT + 3 - j], cw(j), yc, ALU.mult, ALU.add,
                            [("xcb", c), "smalls", yck], [yck])
                    CP("pool", xcb[:, c, 0:3], xcb[:, c, TT:TT + 3], [("xcb", c)], [("xcb", c)])
                    CP("act", ybf[:], yc, [yck], ["ybf"])
                    return
                rb, rbk = sbank()
                ib, ibk = sbank()
                MM(rb[:, :], wrg_sb[:, li, c * 128:(c + 1) * 128], ybf[:], True, True, ["wrg", "ybf"], [rbk])
                MM(ib[:, :], wrg_sb[:, li, 384 + c * 128:384 + (c + 1) * 128], ybf[:], True, True, ["wrg", "ybf"], [ibk])
                ACT(r_, rb[:, :], AF.Sigmoid, [rbk, "smalls"], [rk], bias=sm[:, li, O_BR + c:O_BR + c + 1])
                ACT(ig, ib[:, :], AF.Sigmoid, [ibk, "smalls"], [igk], bias=sm[:, li, O_BI + c:O_BI + c + 1])
                ACT(av, r_, AF.Exp, [rk, "derived"], [avk], scale=derived[:, li, 5 + c:6 + c])
                ACT(mu, r_, AF.Exp, [rk, "derived"], [muk], scale=derived[:, li, 8 + c:9 + c])
                ACT(mu, mu, AF.Sqrt, [muk], [muk], scale=-1.0, bias=1.0)
                TTo("pool", av, av, nm, ALU.mult, [avk, nmk], [avk])
                TTo("pool", mu, mu, nm, ALU.mult, [muk, nmk], [muk])
                TTo("pool", mu, mu, m0, ALU.add, [muk, m0k], [muk])
                TTo("dve", ig, ig, yc, ALU.mult, [igk, yck], [igk])
                TTo("dve", ig, ig, mu, ALU.mult, [igk, muk], [igk])
                P.op("dve", (lambda hs=hs, av=av, ig=ig, c=c: nc.vector.tensor_tensor_scan(
                    out=hs, data0=av, data1=ig, initial=hstate[:, c:c + 1], op0=ALU.mult, op1=ALU.add)),
                    [avk, igk, ("hstate", c)], [hsk])
                CP("act", hstate[:, c:c + 1], hs[:, TT - 1:TT], [hsk], [("hstate", c)])
                TTo("pool", mix[:, 5 + c, :], hs, gl[:, c, :], ALU.mult, [hsk, ("gl", c)], [("mix", 5 + c)])

        def w_out(li):
            for u in range(2):
                wsl, wk = Wnext(("wo", li, u))
                wov = wsl.rearrange("p (o k m) -> p o k m", o=4, k=8)
                for mc in range(4):
                    m = 4 * u + mc
                    bk, bkk = bank()
                    for kc in range(8):
                        MM(bk[:, :], wov[:, mc, kc, :], mix[:, kc, :], kc == 0, kc == 7, [wk, ("mix", kc)], [bkk])
                    xa, xk = X(m)
                    TTo("dve", xa, bk[:, :], xa, ALU.add, [bkk, xk], [xk])

        def state_update(n):
            s = n % 4
            CP("pool", Kring[:, :, s * TT:(s + 1) * TT], Kcur[:, :, :], [("kcur", i) for i in range(4)], ["kring"])
            CP("pool", V1[:, 0, :], V1[:, 4, :], [("v1", 4)], [("v1", 0)])

        def xload(li_, n_):
            b = (li_ * NT + n_) % 2
            src = xT_v if li_ == 0 else xs_v
            T0_ = n_ * TT
            P.dma("sp", (lambda: nc.sync.dma_start(out=x32[:, b], in_=src[:, :, T0_:T0_ + TT])),
                  [("xs", n_)], XKEYS(b))

        for li in range(NL):
            P.op("pool", lambda: nc.gpsimd.memset(hstate[:], 0.0), [("hstate", c) for c in range(3)],
                 [("hstate", c) for c in range(3)])
            P.op("pool", lambda: nc.gpsimd.memset(xcb[:, :, 0:3], 0.0), [], [("xcb", c) for c in range(3)])
            last = li == NL - 1
            for n in range(NT):
                T0 = n * TT
                gidx = li * NT + n
                xcur[0] = gidx % 2
                if li + 1 < NL:
                    nxt = layer_units(li + 1)
                    if n == NT - 1:
                        todo = nxt[n * CAST_PER_TILE:]
                    else:
                        todo = nxt[n * CAST_PER_TILE:(n + 1) * CAST_PER_TILE]
                    for u in todo:
                        cast_unit(u)
                if gidx == 0:
                    xload(0, 0)
                rmsnorm(li, O_G1)
                if gidx + 1 < NL * NT:
                    xload((gidx + 1) // NT, (gidx + 1) % NT)
                ffn(li, 0)
                rmsnorm(li, O_GM)
                rope_tables(n)
                v_proj(li, n)
                qk_proj(li, n)
                attention(li, n)
                w_out(li)
                state_update(n)
                rmsnorm(li, O_G2)
                ffn(li, 1)
                if last:
                    if final_norm:
                        rmsnorm(li, 0, final=True)
                    P.dma("sp", (lambda T0=T0, b=xcur[0]: nc.sync.dma_start(out=outT_v[:, :, T0:T0 + TT], in_=x32[:, b])),
                          XKEYS(xcur[0]), [("out", n)])
                else:
                    P.dma("sp", (lambda T0=T0, b=xcur[0]: nc.sync.dma_start(out=xs_v[:, :, T0:T0 + TT], in_=x32[:, b])),
                          XKEYS(xcur[0]), [("xs", n)])
        assert wstate["used"] == len(seq)
        P.emit(sems_eng, sems_dma, block)
    return nc


def _qk_cols():
    HD = 64
    qa0, ka0, va0, qb0, kb0, vb0, xc0, gc0 = 0, 256, 384, 512, 896, 1280, 1664, 2048

    def heads(base, hs):
        return np.concatenate([np.arange(base + h * HD, base + (h + 1) * HD) for h in hs])

    def rot(cols):
        c = cols.reshape(-1, HD)
        return np.concatenate([c[:, 32:], c[:, :32]], axis=1).reshape(-1)

    chunks = []
    for c in range(3):
        p = heads(qb0, [2 * c, 2 * c + 1])
        chunks += [p, rot(p)]
    for c in range(3):
        p = heads(kb0, [2 * c, 2 * c + 1])
        chunks += [p, rot(p)]
    for c in range(2):
        p = heads(qa0, [c, c + 2])
        chunks += [p, rot(p)]
    p = heads(ka0, [0, 1])
    chunks += [p, rot(p)]
    for c in range(3):
        chunks.append(np.arange(xc0 + c * 128, xc0 + (c + 1) * 128))
    for c in range(3):
        chunks.append(np.arange(gc0 + c * 128, gc0 + (c + 1) * 128))
    vcols = np.concatenate([np.arange(va0, va0 + 128), np.arange(vb0, vb0 + 384)])
    return np.concatenate(chunks), vcols


def _mix_rows():
    HD = 64
    a = np.concatenate([np.arange(0, 64), np.arange(128, 192), np.arange(64, 128), np.arange(192, 256)])
    return np.concatenate([a, np.arange(256, 1024)])


def _consts():
    p = np.arange(128)
    inv = (1.0 / (np.float32(10000.0) ** (np.arange(0, 64, 2, dtype=np.float32) / np.float32(64)))).astype(np.float32)
    c = np.zeros((128, 10), np.float32)
    c[:, 0] = inv[p % 32]
    c[:, 1] = np.where((p % 64) < 32, -1.0, 1.0)
    j = np.arange(128)[:, None]
    i = np.arange(128)[None, :]
    m = np.zeros((128, NMASK, 512), np.float32)
    m[:, M_GE] = np.tile((j >= i).astype(np.float32), (1, 4))
    m[:, M_GT] = np.tile((j > i).astype(np.float32), (1, 4))
    m[:, M_LE] = np.tile((j <= i).astype(np.float32), (1, 4))
    same = ((j - i) % 4) == 0
    m[:, M_F16] = np.tile(same.astype(np.float32), (1, 4))
    m[:, M_O16] = np.tile((same & (j >= i)).astype(np.float32), (1, 4))
    m[:, M_C16] = np.tile((same & (j <= i)).astype(np.float32), (1, 4))
    m[:, M_ID, 0:128] = np.eye(128, dtype=np.float32)
    for src, dst in ((M_GE, M_BGE), (M_GT, M_BGT), (M_LE, M_BLE), (M_F16, M_BF16), (M_O16, M_BO16)):
        m[:, dst] = NEG * (1.0 - m[:, src])
    return c, m.reshape(128, NMASK * 512)


def prep_weights(inp, layers):
    f32 = np.float32
    NL = len(layers)
    qcols, vcols = _qk_cols()
    mrows = _mix_rows()
    wgu = np.empty((NL, 2, 22, 128, 2, 8, 128), f32)
    wd = np.empty((NL, 2, 2, 4, 128, 2, 11, 128), f32)
    wq = np.empty((NL, 6, 128, 4, 8, 128), f32)
    wv = np.empty((NL, 128, 8, 512), f32)
    wo = np.empty((NL, 2, 128, 4, 8, 128), f32)
    wrg = np.zeros((NL, 128, 2, 3, 128), f32)
    smalls = np.zeros((NL, 128, NSMALL), f32)
    p = np.arange(128)
    for i, l in enumerate(layers):
        for f, (gn, un, dn) in enumerate((("ffn1_gate", "ffn1_up", "ffn1_down"), ("ffn2_gate", "ffn2_up", "ffn2_down"))):
            for t, nm in enumerate((gn, un)):
                w = np.asarray(inp[nm][l], f32).reshape(8, 128, 22, 128)
                wgu[i, f, :, :, t] = w.transpose(2, 1, 0, 3)
            w = np.asarray(inp[dn][l], f32).reshape(2, 11, 128, 4, 2, 128)
            wd[i, f] = w.transpose(0, 3, 2, 4, 1, 5)
        win = np.asarray(inp["w_in"][l], f32)
        w = win[:, qcols].reshape(8, 128, 6, 4, 128)
        wq[i] = w.transpose(2, 1, 3, 0, 4)
        wv[i] = win[:, vcols].reshape(8, 128, 512).transpose(1, 0, 2)
        w = np.asarray(inp["w_out"][l], f32)[mrows].reshape(8, 128, 2, 4, 128)
        wo[i] = w.transpose(2, 1, 3, 0, 4)
        for t, nm in enumerate(("rg_w_r", "rg_w_i")):
            w = np.asarray(inp[nm][l], f32)
            for c in range(3):
                wrg[i, 0:64, t, c, 0:64] = w[2 * c]
                wrg[i, 64:128, t, c, 64:128] = w[2 * c + 1]
        for off, nm in ((O_G1, "norm_ffn1"), (O_GM, "norm_mix"), (O_G2, "norm_ffn2")):
            smalls[i, :, off:off + 8] = np.asarray(inp[nm][l], f32).reshape(8, 128).T
        sk = np.asarray(inp["attn_sinks"][l], f32)
        for c in range(2):
            smalls[i, :, O_SINK + c] = sk[c + 2 * (p // 64)]
        cw = np.asarray(inp["conv_w"][l], f32)
        for j in range(4):
            smalls[i, :, O_CW + 3 * j:O_CW + 3 * j + 3] = cw[j].reshape(3, 128).T
        smalls[i, :, O_CB:O_CB + 3] = np.asarray(inp["conv_b"][l], f32).reshape(3, 128).T
        smalls[i, :, O_BR:O_BR + 3] = np.asarray(inp["rg_b_r"][l], f32).reshape(3, 128).T
        smalls[i, :, O_BI:O_BI + 3] = np.asarray(inp["rg_b_i"][l], f32).reshape(3, 128).T
        smalls[i, :, O_LAM:O_LAM + 3] = np.asarray(inp["rg_lambda"][l], f32).reshape(3, 128).T
    consts, masks = _consts()
    consts[:, 2:10] = np.asarray(inp["norm_final"], f32).reshape(8, 128).T
    return dict(wgu=wgu.reshape(NL, 2, 22, 128, 2048), wd=wd.reshape(NL, 2, 2, 4, 128, 2816),
                wq=wq.reshape(NL, 6, 128, 4096), wv=wv.reshape(NL, 128, 4096), wo=wo.reshape(NL, 2, 128, 4096),
                wrg=wrg.reshape(NL, 128, 768), smalls=smalls, consts=consts, masks=masks)


_PROG_CACHE = {}


def _get_prog(S, NL, final_norm):
    key = (S, NL, final_norm)
    if key not in _PROG_CACHE:
        _PROG_CACHE[key] = build_program(S, NL, True, final_norm)
    return _PROG_CACHE[key]


FUSED = True


def kernel(**inp):
    x = np.asarray(inp["x"], np.float32)
    B, S, D = x.shape
    positions = np.asarray(inp["positions"], np.int32)
    depth = np.asarray(inp["norm_ffn1"]).shape[0]
    xT = [np.ascontiguousarray(x[b].T) for b in range(B)]
    groups = [list(range(depth))] if FUSED else [[l] for l in range(depth)]
    for gi, layers in enumerate(groups):
        final = gi == len(groups) - 1
        nc = _get_prog(S, len(layers), final)
        w = prep_weights(inp, layers)
        in_maps = []
        for b in range(B):
            m = dict(w)
            m["xT"] = xT[b]
            m["pos"] = np.ascontiguousarray(positions[b][None, :])
            in_maps.append(m)
        res = run_bass_kernel_spmd(nc, in_maps, core_ids=list(range(B)))
        xT = [np.ascontiguousarray(res.results[b]["outT"]) for b in range(B)]
    return np.stack([xT[b].T for b in range(B)], axis=0).astype(np.float32)
```
